# Optimizing a Trainium2 kernel written in Bass

```python
import jax, jax.numpy as jnp
from jax import lax
import numpy as np

D_MODEL = 1024
BATCH = 2
SEQ = 8192
DEPTH = 2

N_MIXERS = 2

HG_HEADS = 8
HG_DK = D_MODEL // HG_HEADS
HG_DV = D_MODEL // HG_HEADS
HG_KDIM = HG_HEADS * HG_DK
HG_VDIM = HG_HEADS * HG_DV
HG_CHUNK = 64
N_HGRN = (DEPTH + 1) // 2

LRU_WIDTH = D_MODEL
LRU_BLOCKS = 4
LRU_BW = LRU_WIDTH // LRU_BLOCKS
CONV_WIDTH = 4
LRU_C = 8.0
N_LRU = DEPTH // 2

D_FF = ((8 * D_MODEL // 3 + 255) // 256) * 256

EPS = 1e-6

kernel_name = "hgrn2_rglru_interleaved_trunk"


def rmsnorm(x, w):
    xf = x.astype(jnp.float32)
    y = xf * lax.rsqrt(jnp.mean(xf * xf, axis=-1, keepdims=True) + EPS) * w.astype(jnp.float32)
    return y.astype(x.dtype)


def hgrn2_mixer(h, w_in, lb, norm_w, w_out):
    B, S, _ = h.shape
    C = HG_CHUNK
    nc = S // C
    proj = h @ w_in
    q, fl, v, g = jnp.split(proj, [HG_KDIM, 2 * HG_KDIM, 2 * HG_KDIM + HG_VDIM], axis=-1)
    q = jax.nn.silu(q.astype(jnp.float32))
    fl = fl.astype(jnp.float32)
    lbv = lb.astype(jnp.float32)
    log_f = jnp.logaddexp(jnp.log(lbv), jnp.log1p(-lbv) + jax.nn.log_sigmoid(fl))
    k = (1.0 - lbv) * jax.nn.sigmoid(-fl)

    def to_chunks(t, d):
        return t.reshape(B, nc, C, HG_HEADS, d).transpose(1, 0, 3, 2, 4)

    qc = to_chunks(q, HG_DK)
    kc = to_chunks(k, HG_DK)
    vc = to_chunks(v.astype(jnp.float32), HG_DV)
    bc = jnp.cumsum(to_chunks(log_f, HG_DK), axis=3)
    causal = jnp.tril(jnp.ones((C, C), dtype=bool))[:, :, None]

    def step(state, inp):
        qb, kb, vb, bb = inp
        rel = bb[:, :, :, None, :] - bb[:, :, None, :, :]
        decay = jnp.exp(jnp.where(causal, rel, -jnp.inf))
        scores = jnp.einsum('bhtk,bhtsk,bhsk->bhts', qb, decay, kb)
        o = jnp.einsum('bhts,bhsv->bhtv', scores, vb) + jnp.einsum('bhtk,bhkv->bhtv', qb * jnp.exp(bb), state)
        b_last = bb[:, :, -1, :]
        state = jnp.exp(b_last)[..., None] * state + jnp.einsum(
            'bhsk,bhsv->bhkv', kb * jnp.exp(b_last[:, :, None, :] - bb), vb)
        return state, o

    s0 = jnp.zeros((B, HG_HEADS, HG_DK, HG_DV), jnp.float32)
    _, oc = lax.scan(step, s0, (qc, kc, vc, bc))
    o = oc.transpose(1, 0, 3, 2, 4).reshape(B, S, HG_HEADS, HG_DV)
    o = o * lax.rsqrt(jnp.mean(o * o, axis=-1, keepdims=True) + EPS) * norm_w.astype(jnp.float32)
    o = o.reshape(B, S, HG_VDIM) * jax.nn.silu(g.astype(jnp.float32))
    return o.astype(h.dtype) @ w_out


def rglru_mixer(h, w_in, conv_w, conv_b, wa, ba, wx, bx, lam, w_out):
    B, S, _ = h.shape
    proj = h @ w_in
    y, u = jnp.split(proj, [LRU_WIDTH], axis=-1)
    y = jax.nn.gelu(y.astype(jnp.float32))
    u = u.astype(jnp.float32)
    u_pad = jnp.pad(u, ((0, 0), (CONV_WIDTH - 1, 0), (0, 0)))
    cw = conv_w.astype(jnp.float32)
    uc = conv_b.astype(jnp.float32)
    for tap in range(CONV_WIDTH):
        uc = uc + u_pad[:, tap:tap + S, :] * cw[tap]
    ub = uc.reshape(B, S, LRU_BLOCKS, LRU_BW)
    r = jax.nn.sigmoid(jnp.einsum('bsnc,ncd->bsnd', ub, wa.astype(jnp.float32)).reshape(B, S, LRU_WIDTH)
                       + ba.astype(jnp.float32))
    ig = jax.nn.sigmoid(jnp.einsum('bsnc,ncd->bsnd', ub, wx.astype(jnp.float32)).reshape(B, S, LRU_WIDTH)
                        + bx.astype(jnp.float32))
    log_a = -LRU_C * r * jax.nn.softplus(-lam.astype(jnp.float32))
    a = jnp.exp(log_a)
    b_in = jnp.sqrt(-jnp.expm1(2.0 * log_a)) * (ig * uc)

    def combine(left, right):
        a1, b1 = left
        a2, b2 = right
        return a1 * a2, a2 * b1 + b2

    _, hs = lax.associative_scan(combine, (a, b_in), axis=1)
    return (hs * y).astype(h.dtype) @ w_out


def swiglu(h, w_in, w_out):
    gate, up = jnp.split(h @ w_in, [D_FF], axis=-1)
    return (jax.nn.silu(gate) * up) @ w_out


def setup_inputs(seed: int = 0) -> dict:
    key = jax.random.key(seed)
    ks = jax.random.split(key, 24)
    f32 = jnp.float32
    D = D_MODEL

    def nrm(k, shape, fan_in):
        return jax.random.normal(k, shape, f32) * (fan_in ** -0.5)

    x = jax.random.normal(ks[0], (BATCH, SEQ, D), f32)
    norm_mix = 1.0 + 0.02 * jax.random.normal(ks[1], (DEPTH, D), f32)
    norm_ffn = 1.0 + 0.02 * jax.random.normal(ks[2], (DEPTH, D), f32)
    norm_final = 1.0 + 0.02 * jax.random.normal(ks[3], (D,), f32)

    hgrn_w_in = nrm(ks[4], (N_HGRN, D, 2 * HG_KDIM + 2 * HG_VDIM), D)
    hgrn_lb = 0.1 * jax.random.normal(ks[5], (N_HGRN + 1, HG_KDIM), f32)
    hgrn_norm = 1.0 + 0.02 * jax.random.normal(ks[6], (N_HGRN, HG_DV), f32)
    hgrn_w_out = nrm(ks[7], (N_HGRN, HG_VDIM, D), HG_VDIM)

    lru_w_in = nrm(ks[8], (N_LRU, D, 2 * LRU_WIDTH), D)
    lru_conv_w = nrm(ks[9], (N_LRU, CONV_WIDTH, LRU_WIDTH), CONV_WIDTH)
    lru_conv_b = 0.01 * jax.random.normal(ks[10], (N_LRU, LRU_WIDTH), f32)
    lru_wa = nrm(ks[11], (N_LRU, LRU_BLOCKS, LRU_BW, LRU_BW), LRU_BW)
    lru_ba = 0.01 * jax.random.normal(ks[12], (N_LRU, LRU_WIDTH), f32)
    lru_wx = nrm(ks[13], (N_LRU, LRU_BLOCKS, LRU_BW, LRU_BW), LRU_BW)
    lru_bx = 0.01 * jax.random.normal(ks[14], (N_LRU, LRU_WIDTH), f32)
    a0 = jax.random.uniform(ks[15], (N_LRU, LRU_WIDTH), f32, 0.9, 0.999)
    s = a0 ** (1.0 / LRU_C)
    lru_lambda = jnp.log(s) - jnp.log1p(-s)
    lru_w_out = nrm(ks[16], (N_LRU, LRU_WIDTH, D), LRU_WIDTH)

    ffn_w_in = nrm(ks[17], (DEPTH, D, 2 * D_FF), D)
    ffn_w_out = nrm(ks[18], (DEPTH, D_FF, D), D_FF)

    return {"x": x, "norm_mix": norm_mix, "norm_ffn": norm_ffn, "norm_final": norm_final,
            "hgrn_w_in": hgrn_w_in, "hgrn_lb": hgrn_lb, "hgrn_norm": hgrn_norm, "hgrn_w_out": hgrn_w_out,
            "lru_w_in": lru_w_in, "lru_conv_w": lru_conv_w, "lru_conv_b": lru_conv_b,
            "lru_wa": lru_wa, "lru_ba": lru_ba, "lru_wx": lru_wx, "lru_bx": lru_bx,
            "lru_lambda": lru_lambda, "lru_w_out": lru_w_out,
            "ffn_w_in": ffn_w_in, "ffn_w_out": ffn_w_out}


def reference(x, norm_mix, norm_ffn, norm_final, hgrn_w_in, hgrn_lb, hgrn_norm, hgrn_w_out,
              lru_w_in, lru_conv_w, lru_conv_b, lru_wa, lru_ba, lru_wx, lru_bx, lru_lambda, lru_w_out,
              ffn_w_in, ffn_w_out):
    lb_all = jnp.cumsum(jax.nn.softmax(hgrn_lb.astype(jnp.float32), axis=0), axis=0)
    for l in range(DEPTH):
        j = l // N_MIXERS
        hn = rmsnorm(x, norm_mix[l])
        if l % N_MIXERS == 0:
            mix = hgrn2_mixer(hn, hgrn_w_in[j], lb_all[j], hgrn_norm[j], hgrn_w_out[j])
        else:
            mix = rglru_mixer(hn, lru_w_in[j], lru_conv_w[j], lru_conv_b[j], lru_wa[j], lru_ba[j],
                              lru_wx[j], lru_bx[j], lru_lambda[j], lru_w_out[j])
        x = x + mix
        x = x + swiglu(rmsnorm(x, norm_ffn[l]), ffn_w_in[l], ffn_w_out[l])
    return rmsnorm(x, norm_final)
```

```python
import numpy as np
from contextlib import ExitStack
import concourse.bass as bass
import concourse.mybir as mybir
from concourse.bass_utils import run_bass_kernel_spmd

F32 = mybir.dt.float32
BF16 = mybir.dt.bfloat16
AF = mybir.ActivationFunctionType
ALU = mybir.AluOpType
AX = mybir.AxisListType

D = 1024
DFF = 2816
EPS = 1e-6
NSLOT = 3
NPS = 8
STAGE = 5
SLOT = 5632


class Sched:
    ENGS = ("pe", "act", "dve", "pool", "sp")

    def __init__(self, nc, stack, n_dma_lanes=16):
        self.nc = nc
        self.prog = {e: [] for e in self.ENGS}
        self.sem = {}
        for e in ("pe", "act", "dve", "pool"):
            self.sem[e] = stack.enter_context(nc.semaphore("s_" + e))
        self.cnt = {e: 0 for e in self.sem}
        self.known = {e: {} for e in self.ENGS}
        self.lw = {}
        self.rd = {}
        self.lanes = []
        for i in range(n_dma_lanes):
            s = stack.enter_context(nc.semaphore("s_dma%d" % i))
            self.lanes.append([s, 0])
        self.lane_rr = 0
        self.csem = stack.enter_context(nc.semaphore("s_cc"))
        self.ccnt = 0

    def _collect(self, eng, reads, writes):
        deps = {}
        own = id(self.sem[eng]) if eng in self.sem else None

        def add(ev):
            if ev is None:
                return
            s, v, sid = ev
            if eng == "pe" and sid == own:
                return
            if deps.get(sid, (None, 0))[1] < v:
                deps[sid] = (s, v)

        for k in reads:
            add(self.lw.get(k))
        for k in writes:
            add(self.lw.get(k))
            for ev in self.rd.get(k, ()):
                add(ev)
        waits = []
        kn = self.known[eng]
        for sid, (s, v) in deps.items():
            if kn.get(sid, 0) < v:
                kn[sid] = v
                waits.append((s, v))
        return waits

    def _commit(self, ev, reads, writes):
        for k in writes:
            self.lw[k] = ev
            self.rd[k] = []
        for k in reads:
            self.rd.setdefault(k, []).append(ev)

    def op(self, eng, fn, reads=(), writes=()):
        reads = tuple(reads)
        writes = tuple(writes)
        waits = self._collect(eng, reads, writes)
        self.cnt[eng] += 1
        n = self.cnt[eng]
        sem = self.sem[eng]

        def emit(e, waits=waits, fn=fn, sem=sem):
            for (s, v) in waits:
                e.wait_ge(s, v)
            fn(e).then_inc(sem, 1)

        self.prog[eng].append(emit)
        self._commit((sem, n, id(sem)), reads, writes)

    def dma(self, q, out, in_, reads=(), writes=()):
        reads = tuple(reads)
        writes = tuple(writes)
        waits = self._collect(q, reads, writes)
        lane = self.lanes[self.lane_rr]
        self.lane_rr = (self.lane_rr + 1) % len(self.lanes)
        sid = id(lane[0])
        if lane[1] > 0 and self.known[q].get(sid, 0) < lane[1]:
            self.known[q][sid] = lane[1]
            waits.append((lane[0], lane[1]))
        lane[1] += 16
        sem, val = lane[0], lane[1]

        def emit(e, waits=waits, sem=sem, out=out, in_=in_):
            for (s, v) in waits:
                e.wait_ge(s, v)
            e.dma_start(out=out, in_=in_).then_inc(sem, 16)

        self.prog[q].append(emit)
        self._commit((sem, val, sid), reads, writes)

    def coll(self, fn, reads=(), writes=()):
        reads = tuple(reads)
        writes = tuple(writes)
        waits = self._collect("pool", reads, writes)
        self.ccnt += 1
        sem, val = self.csem, self.ccnt

        def emit(e, waits=waits, sem=sem, fn=fn):
            for (s, v) in waits:
                e.wait_ge(s, v)
            fn(e).then_inc(sem, 1)

        self.prog["pool"].append(emit)
        self._commit((sem, val, id(sem)), reads, writes)

    def barrier(self):
        self.nbar = getattr(self, "nbar", 0) + 1
        targets = [(self.sem[e], self.cnt[e]) for e in self.sem if self.cnt[e] > 0]
        targets += [(l[0], l[1]) for l in self.lanes if l[1] > 0]
        for eng in self.ENGS:
            if eng == "pool":
                continue
            waits = []
            kn = self.known[eng]
            for (s, v) in targets:
                if kn.get(id(s), 0) < v:
                    kn[id(s)] = v
                    waits.append((s, v))

            def emit(e, waits=waits):
                for (s, v) in waits:
                    e.wait_ge(s, v)

            self.prog[eng].append(emit)
        keep = lambda k: (isinstance(k, tuple) and k and k[0] == "ring") or (isinstance(k, str) and k.startswith("ag"))
        self.lw = {k: v for k, v in self.lw.items() if keep(k)}
        self.rd = {k: v for k, v in self.rd.items() if keep(k)}

    def emit_all(self):
        nc = self.nc
        with nc.Block() as block:
            @block.tensor
            def _(e):
                for f in self.prog["pe"]:
                    f(e)

            @block.scalar
            def _(e):
                for f in self.prog["act"]:
                    f(e)

            @block.vector
            def _(e):
                for f in self.prog["dve"]:
                    f(e)

            @block.gpsimd
            def _(e):
                for f in self.prog["pool"]:
                    f(e)

            @block.sync
            def _(e):
                for f in self.prog["sp"]:
                    f(e)


def build(T, NCK):
    QL = T * NCK
    NT = T // 128
    NTQ = QL // 128
    TB = min(512, T)
    NTB = T // TB
    TPB = TB // 128
    TS = min(512, T)
    NS = TS // 128
    NSUB = T // TS
    nc = bass.Bass("TRN2", target_bir_lowering=False)

    def din(name, shape):
        return nc.dram_tensor(name, list(shape), F32, kind="ExternalInput").ap()

    x_d = din("x", [QL, D])
    out_d = nc.dram_tensor("out", [QL, D], F32, kind="ExternalOutput").ap()
    w_hin = din("w_hin", [8 * 128, 4096])
    w_hout = din("w_hout", [2 * 128, 4096])
    w_fin = din("w_fin", [2 * 22 * 128, 2048])
    w_fout = din("w_fout", [2 * 2 * 2 * 128, SLOT])
    w_lin = din("w_lin", [4 * 128, 4096])
    w_lout = din("w_lout", [2 * 128, 4096])
    w_ax = din("w_ax", [2 * 128, 2048])
    c_mats = din("c_mats", [128, 5 * 128 + 4])
    c_nrm = din("c_nrm", [128, 32])
    c_lv = din("c_lv", [128, 64])
    c_lb = din("c_lb", [128, 2048])
    c_hnw = din("c_hnw", [128, 128])
    c_nfin = din("c_nfin", [128, 1024])
    c_msk = din("c_msk", [128, 16])
    c_nrmb = din("c_nrmb", [4 * 128, 1024])

    def dint(name, shape):
        return nc.dram_tensor(name, list(shape), F32, kind="Internal").ap()

    NG1 = 8 * 128 + 8
    ag1_in = dint("ag1_in", [128, NG1])
    ag1_out = dint("ag1_out", [4 * 128, NG1])
    ag2_in = dint("ag2_in", [128, 24])
    ag2_out = dint("ag2_out", [4 * 128, 24])
    ag3_in = dint("ag3_in", [128, 16])
    ag3_out = dint("ag3_out", [4 * 128, 16])
    zd = nc.dram_tensor("zd", [8 * 128, QL], BF16, kind="Internal").ap()
    ks_d = nc.dram_tensor("ks_d", [2 * NTQ * 128, 512], F32, kind="Internal").ap()
    vs_d = nc.dram_tensor("vs_d", [2 * NTQ * 128, 512], BF16, kind="Internal").ap()
    RG = [[0, 1, 2, 3], [4, 5, 6, 7]]

    with ExitStack() as st:
        S = Sched(nc, st)

        def sb(name, shape, dt=F32):
            return st.enter_context(nc.sbuf_tensor(name, list(shape), dt))

        x_sb = sb("x_sb", [128, NTQ, D])
        hnT = sb("hnT", [128, 8, T], BF16)
        ring = [sb("ring%d" % i, [128, SLOT], BF16) for i in range(NSLOT)]
        ARENA = 17400
        arena = sb("arena", [128, ARENA])
        narena = sb("narena", [128, 2 * D + 8])
        mats = sb("mats", [128, 5 * 128 + 4])
        lv = sb("lv", [128, 64])
        scl = sb("scl", [128, 32])
        oml = sb("oml", [128, D])
        hnw = sb("hnw", [128, 128])
        nfin = sb("nfin", [128, D])
        Sst = sb("Sst", [128, NG1])
        msk = sb("msk", [128, 16])
        identb = sb("identb", [128, 128], BF16)
        matsb = sb("matsb", [128, 388], BF16)
        sm = sb("sm", [128, 192])
        pst = [st.enter_context(nc.psum_tensor("ps%d" % i, [128, 512], F32)) for i in range(NPS)]

        ident = mats[:, 0:128]
        Lm = mats[:, 128:256]
        Um = mats[:, 256:384]
        maskT = mats[:, 384:512]
        blk = mats[:, 512:514]
        Lmb = matsb[:, 0:128]
        Umb = matsb[:, 128:256]
        blkb = matsb[:, 256:258]
        Ufb = matsb[:, 258:386]
        oneb = matsb[:, 386:388]
        Dt = Sst[:, 1024:1032]
        hloc = sm[:, 0:8]
        pcs = sm[:, 8:16]
        hin = sm[:, 16:24]
        halo = sm[:, 24:48]
        dm = sm[:, 48:56]

        ps_i = [0]

        NROT = NPS - 2
        rot = [NROT]

        def ps():
            i = ps_i[0] % rot[0]
            ps_i[0] += 1
            return pst[i], ("ps", i)

        def ps_pin(j):
            return pst[NROT + j], ("ps", NROT + j)

        def ps_pin_u(j):
            return pst[NPS - 4 + j], ("ps", NPS - 4 + j)

        ring_i = [0]

        def wload(src, nelem, inner):
            i = ring_i[0] % NSLOT
            ring_i[0] += 1
            key = ("ring", i)
            dst = ring[i][:, 0:nelem].rearrange("p (a b) -> p a b", b=inner)
            S.dma("pool", dst, src.rearrange("p (a b) -> p a b", b=inner), writes=[key])
            return ring[i], key

        ar_off = [0]
        arena_tag = [None]

        _raw_barrier = S.barrier

        def _tagged_barrier():
            _raw_barrier()
            arena_tag[0] = None

        S.barrier = _tagged_barrier

        def phase_barrier(tag):
            if arena_tag[0] != tag:
                S.barrier()
                arena_tag[0] = tag

        def carve(nwords, dt=F32):
            o = ar_off[0]
            ar_off[0] += nwords
            assert ar_off[0] <= ARENA, ar_off[0]
            a = arena[:, o:o + nwords]
            return a.bitcast(BF16) if dt == BF16 else a

        S.dma("sp", mats[:], c_mats, writes=["mats"])
        S.dma("sp", lv[:], c_lv, writes=["lv"])
        S.dma("sp", hnw[:], c_hnw, writes=["hnw"])
        S.dma("sp", nfin[:], c_nfin, writes=["nfin"])
        S.dma("sp", msk[:], c_msk, writes=["msk"])
        S.dma("sp", arena[:, 0:2048], c_lb, writes=["lbtmp"])
        for g in range(NTQ):
            S.dma("sp", x_sb[:, g, :], x_d[g * 128:(g + 1) * 128, :], writes=[("x", g)])
        S.op("dve", lambda e: e.tensor_tensor(out=oml[:], in0=arena[:, 1024:2048], in1=arena[:, 0:1024],
                                              op=ALU.subtract), reads=["lbtmp"], writes=["oml"])
        S.op("act", lambda e: e.activation(out=oml[:], in_=oml[:], func=AF.Sigmoid), reads=["oml"], writes=["oml"])
        lam = lv[:].rearrange("p (c v) -> p c v", v=8)[:, :, 7]
        S.op("act", lambda e: e.activation(out=scl[:, 0:8], in_=lam, func=AF.Exp, scale=-1.0), reads=["lv"], writes=["scl"])
        S.op("act", lambda e: e.activation(out=scl[:, 0:8], in_=scl[:, 0:8], func=AF.Ln, bias=1.0), reads=["scl"], writes=["scl"])
        S.op("dve", lambda e: e.tensor_scalar(out=scl[:, 8:16], in0=scl[:, 0:8], scalar1=-4.0, scalar2=None, op0=ALU.mult),
             reads=["scl"], writes=["scl"])
        S.op("dve", lambda e: e.tensor_scalar(out=scl[:, 0:8], in0=scl[:, 0:8], scalar1=-8.0, scalar2=None, op0=ALU.mult),
             reads=["scl"], writes=["scl"])
        lvv = lv[:].rearrange("p (c v) -> p c v", v=8)
        S.op("dve", lambda e: e.tensor_scalar(out=scl[:, 16:24], in0=lvv[:, :, 5], scalar1=0.5, scalar2=None, op0=ALU.mult),
             reads=["lv", "scl"], writes=["scl"])
        S.op("dve", lambda e: e.tensor_scalar(out=scl[:, 24:32], in0=lvv[:, :, 6], scalar1=0.5, scalar2=None, op0=ALU.mult),
             reads=["lv", "scl"], writes=["scl"])
        S.op("dve", lambda e: e.memset(Sst[:, 0:1024], 0.0), writes=[("S", h) for h in range(8)])
        S.op("dve", lambda e: e.memset(Dt, 1.0), writes=["Dt"])
        S.op("dve", lambda e: e.tensor_copy(out=identb[:], in_=ident), reads=["mats"], writes=["identb"])
        S.op("dve", lambda e: e.tensor_copy(out=matsb[:, 0:256], in_=mats[:, 128:384]), reads=["mats"], writes=["matsb"])
        S.op("dve", lambda e: e.tensor_copy(out=matsb[:, 256:258], in_=blk), reads=["mats"], writes=["matsb"])
        S.op("dve", lambda e: e.tensor_copy(out=matsb[:, 258:388], in_=mats[:, 514:644]), reads=["mats"], writes=["matsb"])
        S.op("dve", lambda e: e.memset(sm[:], 0.0), writes=["sm"])
        S.op("dve", lambda e: e.memset(pcs, 1.0), reads=["sm"], writes=["sm"])
        S.barrier()

        wb = narena[:, 0:D]
        xsb = [narena[:, D:D + 512].bitcast(BF16), narena[:, D + 512:D + 1024].bitcast(BF16)]
        ssq1 = narena[:, 2 * D:2 * D + 8]

        def rmsnorm_T(ni, ck, tiles=None):
            S.dma("sp", wb, c_nrmb[ni * 128:(ni + 1) * 128, :], writes=["wb"])
            tl = list(range(NT) if tiles is None else tiles)
            pend = {}

            def stage_a(i, b):
                g = ck * NT + i
                ss = ssq1[:, 2 * b:2 * b + 1]
                rs = ssq1[:, 2 * b + 1:2 * b + 2]
                S.op("act", lambda e: e.activation(out=xsb[b], in_=x_sb[:, g, :], func=AF.Square, accum_out=ss),
                     reads=[("x", g)], writes=[("xs", b), ("ss", b)])
                S.op("act", lambda e: e.activation(out=rs, in_=ss, func=AF.Sqrt, scale=1.0 / D, bias=EPS),
                     reads=[("ss", b)], writes=[("rs", b)])
                S.op("dve", lambda e: e.reciprocal(out=rs, in_=rs), reads=[("rs", b)], writes=[("rs", b)])
                S.op("dve", lambda e: e.scalar_tensor_tensor(out=xsb[b], in0=x_sb[:, g, :], scalar=rs, in1=wb,
                                                             op0=ALU.mult, op1=ALU.mult),
                     reads=[("x", g), ("rs", b), "wb"], writes=[("xs", b)])
                banks = []
                for kb in range(2):
                    pt, pk = ps()
                    for kk in range(4):
                        k = kb * 4 + kk
                        S.op("pe", lambda e, pt=pt, kk=kk, k=k: e.matmul(pt[:, kk * 128:(kk + 1) * 128],
                                                                         lhsT=xsb[b][:, k * 128:(k + 1) * 128], rhs=identb[:],
                                                                         start=True, stop=True),
                             reads=[("xs", b), "identb"], writes=[pk])
                    banks.append((pt, pk))
                pend[i] = banks

            def stage_b(i):
                for kb, (pt, pk) in enumerate(pend.pop(i)):
                    o = hnT[:, kb * 4:(kb + 1) * 4, i * 128:(i + 1) * 128]
                    src = pt[:].rearrange("p (k t) -> p k t", k=4)
                    wr = [("hn", kb * 4 + kk, i) for kk in range(4)]
                    if kb == 0:
                        S.op("act", lambda e, o=o, src=src: e.activation(out=o, in_=src, func=AF.Copy), reads=[pk], writes=wr)
                    else:
                        S.op("dve", lambda e, o=o, src=src: e.tensor_copy(out=o, in_=src), reads=[pk], writes=wr)

            if rot[0] < 4:
                for n_, i in enumerate(tl):
                    stage_a(i, n_ % 2)
                    stage_b(i)
                return
            stage_a(tl[0], 0)
            for n_, i in enumerate(tl):
                if n_ + 1 < len(tl):
                    stage_a(tl[n_ + 1], (n_ + 1) % 2)
                stage_b(i)

        def out_proj(w_d, srcT, srckey, pairs):
            for half in range(2):
                wt, wk = wload(w_d[half * 128:(half + 1) * 128, :], 4096, 2048)
                for (si, g) in pairs:
                    pt, pk = ps()
                    for k in range(8):
                        S.op("pe", lambda e, pt=pt, k=k, si=si, wt=wt: e.matmul(
                            pt[:], lhsT=srcT[:, k, si * 128:(si + 1) * 128], rhs=wt[:, k * 512:(k + 1) * 512],
                            start=(k == 0), stop=(k == 7)), reads=[wk, (srckey, k, si)], writes=[pk])
                    xo = x_sb[:, g, half * 512:(half + 1) * 512]
                    S.op("dve", lambda e, xo=xo, pt=pt: e.tensor_tensor(out=xo, in0=xo, in1=pt[:], op=ALU.add),
                         reads=[pk, ("x", g)], writes=[("x", g)])

        def drive(*gens):
            gens = [g for g in gens if g is not None]
            while gens:
                for g in list(gens):
                    try:
                        next(g)
                    except StopIteration:
                        gens.remove(g)

        def v3(a):
            return a.rearrange("p (h d) -> p h d", h=4)

        def hgrn_sub(ck, s, state_only=False):
            phase_barrier("hgrn")
            ar_off[0] = 0
            kst = carve(NS * 512).rearrange("p (i c) -> p i c", c=512)
            qs = carve(NS * 256, BF16).rearrange("p (i c) -> p i c", c=512)
            vb = carve(NS * 256, BF16).rearrange("p (i c) -> p i c", c=512)
            gs = carve(NS * 256, BF16).rearrange("p (i c) -> p i c", c=512)
            ogT = carve(4 * TS, BF16).rearrange("p (h t) -> p h t", t=TS)
            logf = carve(512)
            lhi = carve(256, BF16)
            llo = carve(256, BF16)
            eb = [carve(512), carve(512)]
            enb = [carve(512), carve(512)]
            eR = [carve(512), carve(512)]
            qe = carve(256, BF16)
            ke = carve(256, BF16)
            keT = carve(256, BF16)
            kd = [carve(256, BF16), carve(256, BF16)]
            qeT = [carve(256, BF16), carve(256, BF16)]
            scT = [carve(256, BF16), carve(256, BF16)]
            dS = [carve(8), carve(8), carve(8)]
            sq = carve(512)
            on = carve(512)
            og = carve(256, BF16)
            SBf = carve(512)
            SAb = carve(256, BF16)
            SBb = carve(256, BF16)
            ssq = carve(4)
            rst = carve(4)
            i0 = s * NS
            for hg in range(2):
                for il in range(NS):
                    r0 = (hg * NTQ + ck * NT + i0 + il) * 128
                    S.dma("sp", kst[:, il, :], ks_d[r0:r0 + 128, :], writes=[("k", il)])
                    S.dma("sp", vb[:, il, :], vs_d[r0:r0 + 128, :], writes=[("vb", il)])
                for kind in (0, 3):
                    bi = hg * 4 + kind
                    wt, wk = wload(w_hin[bi * 128:(bi + 1) * 128, :], 4096, 2048)
                    for il in range(NS):
                        i = i0 + il
                        pt, pk = ps()
                        for k in range(8):
                            S.op("pe", lambda e, pt=pt, k=k, i=i, wt=wt: e.matmul(
                                pt[:], lhsT=hnT[:, k, i * 128:(i + 1) * 128], rhs=wt[:, k * 512:(k + 1) * 512],
                                start=(k == 0), stop=(k == 7)), reads=[wk, ("hn", k, i)], writes=[pk])
                        if kind == 0:
                            S.op("act", lambda e, pt=pt, il=il: e.activation(out=qs[:, il, :], in_=pt[:], func=AF.Silu),
                                 reads=[pk], writes=[("qs", il)])
                        elif kind == 1:
                            S.op("act", lambda e, pt=pt, il=il: e.activation(out=kst[:, il, :], in_=pt[:], func=AF.Sigmoid, scale=-1.0),
                                 reads=[pk], writes=[("k", il)])
                            S.op("dve", lambda e, il=il, hg=hg: e.tensor_tensor(out=kst[:, il, :], in0=kst[:, il, :],
                                                                               in1=oml[:, hg * 512:(hg + 1) * 512], op=ALU.mult),
                                 reads=[("k", il), "oml"], writes=[("k", il)])
                        elif kind == 2:
                            S.op("dve", lambda e, pt=pt, il=il: e.tensor_copy(out=vb[:, il, :], in_=pt[:]),
                                 reads=[pk], writes=[("vb", il)])
                        else:
                            S.op("act", lambda e, pt=pt, il=il: e.activation(out=gs[:, il, :], in_=pt[:], func=AF.Silu),
                                 reads=[pk], writes=[("gs", il)])

                def f1(i, hg=hg):
                    b = i % 2
                    b3 = i % 3
                    S.op("act", lambda e: e.activation(out=logf, in_=kst[:, i, :], func=AF.Ln, scale=-1.0, bias=1.0),
                         reads=[("k", i)], writes=["logf"])
                    S.op("act", lambda e: e.activation(out=lhi, in_=kst[:, i, :], func=AF.Ln, scale=-1.0, bias=1.0),
                         reads=[("k", i)], writes=["lhi"])
                    yield
                    S.op("dve", lambda e: e.tensor_tensor(out=llo, in0=logf, in1=lhi, op=ALU.subtract),
                         reads=["logf", "lhi"], writes=["llo"])
                    yield
                    pb, pbk = ps()
                    S.op("pe", lambda e: e.matmul(pb[:], lhsT=Lmb, rhs=lhi, start=True, stop=False), reads=["lhi", "matsb"], writes=[pbk])
                    S.op("pe", lambda e: e.matmul(pb[:], lhsT=Lmb, rhs=llo, start=False, stop=True), reads=["llo", "matsb"], writes=[pbk])
                    pr, prk = ps()
                    S.op("pe", lambda e: e.matmul(pr[:], lhsT=Umb, rhs=lhi, start=True, stop=False), reads=["lhi", "matsb"], writes=[prk])
                    S.op("pe", lambda e: e.matmul(pr[:], lhsT=Umb, rhs=llo, start=False, stop=True), reads=["llo", "matsb"], writes=[prk])
                    pd, pdk = ps()
                    for hh in range(4):
                        S.op("pe", lambda e, hh=hh: e.matmul(pd[:, 2 * hh:2 * hh + 2], lhsT=lhi[:, hh * 128:(hh + 1) * 128],
                                                             rhs=blkb, start=True, stop=False), reads=["lhi", "matsb"], writes=[pdk])
                        S.op("pe", lambda e, hh=hh: e.matmul(pd[:, 2 * hh:2 * hh + 2], lhsT=llo[:, hh * 128:(hh + 1) * 128],
                                                             rhs=blkb, start=False, stop=True), reads=["llo", "matsb"], writes=[pdk])
                    S.op("act", lambda e: e.activation(out=eb[b], in_=pb[:], func=AF.Exp), reads=[pbk], writes=[("eb", b)])
                    S.op("act", lambda e: e.activation(out=enb[b], in_=pb[:], func=AF.Exp, scale=-1.0), reads=[pbk], writes=[("enb", b)])
                    S.op("act", lambda e: e.activation(out=eR[b], in_=pr[:], func=AF.Exp), reads=[prk], writes=[("eR", b)])
                    S.op("act", lambda e: e.activation(out=dS[b3], in_=pd[:, 0:8], func=AF.Exp), reads=[pdk], writes=[("dS", b3)])

                def f2(i, hg=hg):
                    b = i % 2
                    S.op("dve", lambda e: e.tensor_tensor(out=kd[b], in0=kst[:, i, :], in1=eR[b], op=ALU.mult),
                         reads=[("k", i), ("eR", b)], writes=[("kd", b)])
                    S.op("dve", lambda e: e.tensor_tensor(out=qe, in0=qs[:, i, :], in1=eb[b], op=ALU.mult),
                         reads=[("qs", i), ("eb", b)], writes=["qe"])
                    yield
                    S.op("dve", lambda e: e.tensor_tensor(out=ke, in0=kst[:, i, :], in1=enb[b], op=ALU.mult),
                         reads=[("k", i), ("enb", b)], writes=["ke"])
                    pq, pqk = ps()
                    for hh in range(4):
                        S.op("pe", lambda e, hh=hh: e.matmul(pq[:, hh * 128:(hh + 1) * 128], lhsT=qe[:, hh * 128:(hh + 1) * 128],
                                                             rhs=identb[:], start=True, stop=True), reads=["qe", "identb"], writes=[pqk])
                    S.op("act", lambda e: e.activation(out=qeT[b], in_=pq[:], func=AF.Copy), reads=[pqk], writes=[("qeT", b)])
                    yield
                    pk_, pkk = ps()
                    for hh in range(4):
                        S.op("pe", lambda e, hh=hh: e.matmul(pk_[:, hh * 128:(hh + 1) * 128], lhsT=ke[:, hh * 128:(hh + 1) * 128],
                                                             rhs=identb[:], start=True, stop=True), reads=["ke", "identb"], writes=[pkk])
                    S.op("dve", lambda e: e.tensor_copy(out=keT, in_=pk_[:]), reads=[pkk], writes=["keT"])
                    yield
                    psc, psck = ps()
                    for hh in range(4):
                        S.op("pe", lambda e, hh=hh: e.matmul(psc[:, hh * 128:(hh + 1) * 128], lhsT=keT[:, hh * 128:(hh + 1) * 128],
                                                             rhs=qeT[b][:, hh * 128:(hh + 1) * 128], start=True, stop=True),
                             reads=["keT", ("qeT", b)], writes=[psck])
                    S.op("dve", lambda e: e.tensor_tensor(out=v3(scT[b]), in0=v3(psc[:]),
                                                          in1=maskT.unsqueeze(1).broadcast_to([128, 4, 128]), op=ALU.mult),
                         reads=[psck, "mats"], writes=[("scT", b)])

                def b1(i, hg=hg):
                    b = i % 2
                    b3 = i % 3
                    pua, puak = ps()
                    pub, pubk = ps()
                    for hh in range(4):
                        cs = slice(hh * 128, (hh + 1) * 128)
                        S.op("pe", lambda e, cs=cs: e.matmul(pua[:, cs], lhsT=kd[b][0:64, cs], rhs=vb[0:64, i, cs], start=True, stop=True),
                             reads=[("kd", b), ("vb", i)], writes=[puak])
                        S.op("pe", lambda e, cs=cs: e.matmul(pub[:, cs], lhsT=kd[b][64:128, cs], rhs=vb[64:128, i, cs], start=True, stop=True),
                             reads=[("kd", b), ("vb", i)], writes=[pubk])
                    hkeys4 = [("S", hg * 4 + hh) for hh in range(4)]
                    S.op("act", lambda e: e.activation(out=SAb, in_=Sst[:, hg * 512:(hg + 1) * 512], func=AF.Copy),
                         reads=hkeys4, writes=[("SAb", hh) for hh in range(4)])
                    for hh in range(4):
                        h = hg * 4 + hh
                        cs = slice(hh * 128, (hh + 1) * 128)
                        Sh = Sst[:, h * 128:(h + 1) * 128]
                        S.op("dve", lambda e, cs=cs, Sh=Sh, hh=hh: e.scalar_tensor_tensor(
                            out=SBf[:, cs], in0=Sh, scalar=dS[b3][:, 2 * hh:2 * hh + 1], in1=pua[:, cs], op0=ALU.mult, op1=ALU.add),
                            reads=[("S", h), ("dS", b3), puak, ("SAb", hh)], writes=[("SBf", hh)])
                        S.op("dve", lambda e, cs=cs, Sh=Sh, hh=hh: e.scalar_tensor_tensor(
                            out=Sh, in0=SBf[:, cs], scalar=dS[b3][:, 2 * hh + 1:2 * hh + 2], in1=pub[:, cs], op0=ALU.mult, op1=ALU.add),
                            reads=[("SBf", hh), ("dS", b3), pubk], writes=[("S", h)])
                    S.op("act", lambda e: e.activation(out=SBb, in_=SBf, func=AF.Copy),
                         reads=[("SBf", hh) for hh in range(4)], writes=[("SBb", hh) for hh in range(4)])
                    yield
                    po, pok = ps_pin(b)
                    for hh in range(4):
                        cs = slice(hh * 128, (hh + 1) * 128)
                        S.op("pe", lambda e, cs=cs: e.matmul(po[:, cs], lhsT=scT[b][:, cs], rhs=vb[:, i, cs], start=True, stop=False),
                             reads=[("scT", b), ("vb", i)], writes=[pok])
                        S.op("pe", lambda e, cs=cs, hh=hh: e.matmul(po[0:64, cs], lhsT=qeT[b][:, hh * 128:hh * 128 + 64],
                                                                    rhs=SAb[:, cs], start=False, stop=False),
                             reads=[("qeT", b), ("SAb", hh)], writes=[pok])
                        S.op("pe", lambda e, cs=cs, hh=hh: e.matmul(po[64:128, cs], lhsT=qeT[b][:, hh * 128 + 64:hh * 128 + 128],
                                                                    rhs=SBb[:, cs], start=False, stop=True),
                             reads=[("qeT", b), ("SBb", hh)], writes=[pok])

                def b2(i, hg=hg):
                    po, pok = ps_pin(i % 2)
                    S.op("act", lambda e: e.activation(out=sq, in_=po[:], func=AF.Square), reads=[pok], writes=["sq"])
                    yield
                    S.op("dve", lambda e: e.tensor_reduce(out=ssq, in_=v3(sq), axis=AX.X, op=ALU.add), reads=["sq"], writes=["ssq"])
                    yield
                    S.op("act", lambda e: e.activation(out=rst, in_=ssq, func=AF.Ln, scale=1.0 / 128, bias=EPS),
                         reads=["ssq"], writes=["rst"])
                    S.op("act", lambda e: e.activation(out=rst, in_=rst, func=AF.Exp, scale=-0.5), reads=["rst"], writes=["rst"])
                    yield
                    S.op("dve", lambda e: e.tensor_tensor(out=v3(on), in0=v3(po[:]),
                                                          in1=rst.unsqueeze(2).broadcast_to([128, 4, 128]), op=ALU.mult),
                         reads=[pok, "rst"], writes=["on"])
                    yield
                    S.op("dve", lambda e: e.tensor_tensor(out=on, in0=on, in1=gs[:, i, :], op=ALU.mult),
                         reads=["on", ("gs", i)], writes=["on"])
                    yield
                    S.op("dve", lambda e: e.tensor_tensor(out=v3(og), in0=v3(on), in1=hnw[:].unsqueeze(1).broadcast_to([128, 4, 128]),
                                                          op=ALU.mult), reads=["on", "hnw"], writes=["og"])
                    yield
                    pg, pgk = ps()
                    for hh in range(4):
                        S.op("pe", lambda e, hh=hh: e.matmul(pg[:, hh * 128:(hh + 1) * 128], lhsT=og[:, hh * 128:(hh + 1) * 128],
                                                             rhs=identb[:], start=True, stop=True), reads=["og", "identb"], writes=[pgk])
                    S.op("act", lambda e: e.activation(out=ogT[:, hg * 4:(hg + 1) * 4, i * 128:(i + 1) * 128],
                                                       in_=v3(pg[:]), func=AF.Copy),
                         reads=[pgk], writes=[("ogT", hg * 4 + hh, i) for hh in range(4)])

                for step in range(NS + 3):
                    gens = []
                    if step < NS:
                        gens.append(f1(step))
                    if 0 <= step - 1 < NS:
                        gens.append(f2(step - 1))
                    if 0 <= step - 2 < NS:
                        gens.append(b1(step - 2))
                    if 0 <= step - 3 < NS:
                        gens.append(b2(step - 3))
                    drive(*gens)
            out_proj(w_hout, ogT, "ogT", [(il, ck * NT + i0 + il) for il in range(NS)])

        def hgrn_full():
            phase_barrier("hgrn")
            ar_off[0] = 0
            rot[0] = NPS - 4
            qs = [carve(NS * 256, BF16).rearrange("p (i c) -> p i c", c=512) for _ in range(2)]
            gs = [carve(NS * 256, BF16).rearrange("p (i c) -> p i c", c=512) for _ in range(2)]
            kt = [carve(512) for _ in range(3)]
            vt = [carve(256, BF16) for _ in range(3)]
            ogT = carve(4 * TS, BF16).rearrange("p (h t) -> p h t", t=TS)
            logf = carve(512)
            lhi = carve(256, BF16)
            llo = carve(256, BF16)
            eb = [carve(512), carve(512)]
            enb = [carve(512), carve(512)]
            eR = [carve(512), carve(512)]
            qe = carve(256, BF16)
            ke = carve(256, BF16)
            keT = carve(256, BF16)
            kd = [carve(256, BF16), carve(256, BF16)]
            qeT = [carve(256, BF16), carve(256, BF16)]
            scT = [carve(256, BF16), carve(256, BF16)]
            dS = [carve(8), carve(8), carve(8)]
            sq = carve(512)
            on = carve(512)
            og = carve(256, BF16)
            SBf = carve(512)
            SAb = carve(256, BF16)
            SBb = carve(256, BF16)
            ssq = carve(4)
            rst = carve(4)
            units = [(ck, s, hg) for ck in range(NCK) for s in range(NSUB) for hg in range(2)]
            NU = len(units)
            items = [(u, il) for u in range(NU) for il in range(NS)]
            NI = len(items)

            def gtile(u, il):
                ck, s, hg = units[u]
                return ck * NT + s * NS + il

            def load_k(j):
                u, il = items[j]
                r0 = (units[u][2] * NTQ + gtile(u, il)) * 128
                S.dma("sp", kt[j % 3], ks_d[r0:r0 + 128, :], writes=[("kt", j % 3)])

            def load_v(j):
                u, il = items[j]
                r0 = (units[u][2] * NTQ + gtile(u, il)) * 128
                S.dma("sp", vt[j % 3], vs_d[r0:r0 + 128, :], writes=[("vt", j % 3)])

            def inproj_groups(u, kind):
                ck, s, hg = units[u]
                p = u % 2
                bi = hg * 4 + kind
                wt, wk = wload(w_hin[bi * 128:(bi + 1) * 128, :], 4096, 2048)
                out = []
                for il in range(NS):
                    def grp(il=il):
                        i = s * NS + il
                        pt, pk = ps()
                        for k in range(8):
                            S.op("pe", lambda e, k=k: e.matmul(pt[:], lhsT=hnT[:, k, i * 128:(i + 1) * 128],
                                                               rhs=wt[:, k * 512:(k + 1) * 512], start=(k == 0), stop=(k == 7)),
                                 reads=[wk, ("hn", k, i)], writes=[pk])
                        dst = qs[p] if kind == 0 else gs[p]
                        S.op("act", lambda e: e.activation(out=dst[:, il, :], in_=pt[:], func=AF.Silu),
                             reads=[pk], writes=[("qs" if kind == 0 else "gs", p, il)])
                        if kind == 3:
                            S.op("dve", lambda e: e.tensor_tensor(out=v3(dst[:, il, :]), in0=v3(dst[:, il, :]),
                                                                  in1=hnw[:].unsqueeze(1).broadcast_to([128, 4, 128]), op=ALU.mult),
                                 reads=[("gs", p, il), "hnw"], writes=[("gs", p, il)])
                    out.append(grp)
                return out

            def f1(j):
                b, b3, kb = j % 2, j % 3, j % 3
                S.op("act", lambda e: e.activation(out=logf, in_=kt[kb], func=AF.Ln, scale=-1.0, bias=1.0),
                     reads=[("kt", kb)], writes=["logf"])
                S.op("act", lambda e: e.activation(out=lhi, in_=kt[kb], func=AF.Ln, scale=-1.0, bias=1.0),
                     reads=[("kt", kb)], writes=["lhi"])
                yield
                S.op("dve", lambda e: e.tensor_tensor(out=llo, in0=logf, in1=lhi, op=ALU.subtract),
                     reads=["logf", "lhi"], writes=["llo"])
                yield
                pb, pbk = ps()
                S.op("pe", lambda e: e.matmul(pb[:], lhsT=Lmb, rhs=lhi, start=True, stop=False), reads=["lhi", "matsb"], writes=[pbk])
                S.op("pe", lambda e: e.matmul(pb[:], lhsT=Lmb, rhs=llo, start=False, stop=True), reads=["llo", "matsb"], writes=[pbk])
                S.op("act", lambda e: e.activation(out=eb[b], in_=pb[:], func=AF.Exp), reads=[pbk], writes=[("eb", b)])
                S.op("act", lambda e: e.activation(out=enb[b], in_=pb[:], func=AF.Exp, scale=-1.0), reads=[pbk], writes=[("enb", b)])
                yield
                pr, prk = ps()
                S.op("pe", lambda e: e.matmul(pr[:], lhsT=Umb, rhs=lhi, start=True, stop=False), reads=["lhi", "matsb"], writes=[prk])
                S.op("pe", lambda e: e.matmul(pr[:], lhsT=Umb, rhs=llo, start=False, stop=True), reads=["llo", "matsb"], writes=[prk])
                S.op("act", lambda e: e.activation(out=eR[b], in_=pr[:], func=AF.Exp), reads=[prk], writes=[("eR", b)])
                yield
                pd, pdk = ps()
                for hh in range(4):
                    S.op("pe", lambda e, hh=hh: e.matmul(pd[:, 2 * hh:2 * hh + 2], lhsT=lhi[:, hh * 128:(hh + 1) * 128],
                                                         rhs=blkb, start=True, stop=False), reads=["lhi", "matsb"], writes=[pdk])
                    S.op("pe", lambda e, hh=hh: e.matmul(pd[:, 2 * hh:2 * hh + 2], lhsT=llo[:, hh * 128:(hh + 1) * 128],
                                                         rhs=blkb, start=False, stop=True), reads=["llo", "matsb"], writes=[pdk])
                S.op("act", lambda e: e.activation(out=dS[b3], in_=pd[:, 0:8], func=AF.Exp), reads=[pdk], writes=[("dS", b3)])

            def f2(j):
                u, il = items[j]
                p, b, kb = u % 2, j % 2, j % 3
                S.op("dve", lambda e: e.tensor_tensor(out=kd[b], in0=kt[kb], in1=eR[b], op=ALU.mult),
                     reads=[("kt", kb), ("eR", b)], writes=[("kd", b)])
                S.op("dve", lambda e: e.tensor_tensor(out=qe, in0=qs[p][:, il, :], in1=eb[b], op=ALU.mult),
                     reads=[("qs", p, il), ("eb", b)], writes=["qe"])
                yield
                S.op("dve", lambda e: e.tensor_tensor(out=ke, in0=kt[kb], in1=enb[b], op=ALU.mult),
                     reads=[("kt", kb), ("enb", b)], writes=["ke"])
                pq, pqk = ps()
                for hh in range(4):
                    S.op("pe", lambda e, hh=hh: e.matmul(pq[:, hh * 128:(hh + 1) * 128], lhsT=qe[:, hh * 128:(hh + 1) * 128],
                                                         rhs=identb[:], start=True, stop=True), reads=["qe", "identb"], writes=[pqk])
                S.op("act", lambda e: e.activation(out=qeT[b], in_=pq[:], func=AF.Copy), reads=[pqk], writes=[("qeT", b)])
                yield
                pk_, pkk = ps()
                for hh in range(4):
                    S.op("pe", lambda e, hh=hh: e.matmul(pk_[:, hh * 128:(hh + 1) * 128], lhsT=ke[:, hh * 128:(hh + 1) * 128],
                                                         rhs=identb[:], start=True, stop=True), reads=["ke", "identb"], writes=[pkk])
                S.op("dve", lambda e: e.tensor_copy(out=keT, in_=pk_[:]), reads=[pkk], writes=["keT"])
                yield
                psc, psck = ps()
                for hh in range(4):
                    S.op("pe", lambda e, hh=hh: e.matmul(psc[:, hh * 128:(hh + 1) * 128], lhsT=keT[:, hh * 128:(hh + 1) * 128],
                                                         rhs=qeT[b][:, hh * 128:(hh + 1) * 128], start=True, stop=True),
                         reads=["keT", ("qeT", b)], writes=[psck])
                S.op("dve", lambda e: e.tensor_tensor(out=v3(scT[b]), in0=v3(psc[:]),
                                                      in1=maskT.unsqueeze(1).broadcast_to([128, 4, 128]), op=ALU.mult),
                     reads=[psck, "mats"], writes=[("scT", b)])

            def b1(j):
                u, il = items[j]
                hg = units[u][2]
                b, b3, vb_ = j % 2, j % 3, j % 3
                v = vt[vb_]
                vk = ("vt", vb_)
                pua, puak = ps_pin_u(0)
                pub, pubk = ps_pin_u(1)
                for hh in range(4):
                    cs = slice(hh * 128, (hh + 1) * 128)
                    S.op("pe", lambda e, cs=cs: e.matmul(pua[:, cs], lhsT=kd[b][0:64, cs], rhs=v[0:64, cs], start=True, stop=True),
                         reads=[("kd", b), vk], writes=[puak])
                    S.op("pe", lambda e, cs=cs: e.matmul(pub[:, cs], lhsT=kd[b][64:128, cs], rhs=v[64:128, cs], start=True, stop=True),
                         reads=[("kd", b), vk], writes=[pubk])
                S.op("act", lambda e: e.activation(out=SAb, in_=Sst[:, hg * 512:(hg + 1) * 512], func=AF.Copy),
                     reads=[("S", hg * 4 + hh) for hh in range(4)], writes=[("SAb", hh) for hh in range(4)])
                yield
                for hh in range(4):
                    h = hg * 4 + hh
                    cs = slice(hh * 128, (hh + 1) * 128)
                    Sh = Sst[:, h * 128:(h + 1) * 128]
                    S.op("dve", lambda e, cs=cs, Sh=Sh, hh=hh: e.scalar_tensor_tensor(
                        out=SBf[:, cs], in0=Sh, scalar=dS[b3][:, 2 * hh:2 * hh + 1], in1=pua[:, cs], op0=ALU.mult, op1=ALU.add),
                        reads=[("S", h), ("dS", b3), puak, ("SAb", hh)], writes=[("SBf", hh)])
                    S.op("dve", lambda e, cs=cs, Sh=Sh, hh=hh: e.scalar_tensor_tensor(
                        out=Sh, in0=SBf[:, cs], scalar=dS[b3][:, 2 * hh + 1:2 * hh + 2], in1=pub[:, cs], op0=ALU.mult, op1=ALU.add),
                        reads=[("SBf", hh), ("dS", b3), pubk], writes=[("S", h)])
                    if hh % 2 == 1:
                        yield
                S.op("act", lambda e: e.activation(out=SBb, in_=SBf, func=AF.Copy),
                     reads=[("SBf", hh) for hh in range(4)], writes=[("SBb", hh) for hh in range(4)])
                po, pok = ps_pin(b)
                for hh in range(4):
                    cs = slice(hh * 128, (hh + 1) * 128)
                    S.op("pe", lambda e, cs=cs: e.matmul(po[:, cs], lhsT=scT[b][:, cs], rhs=v[:, cs], start=True, stop=False),
                         reads=[("scT", b), vk], writes=[pok])
                    S.op("pe", lambda e, cs=cs, hh=hh: e.matmul(po[0:64, cs], lhsT=qeT[b][:, hh * 128:hh * 128 + 64],
                                                                rhs=SAb[:, cs], start=False, stop=False),
                         reads=[("qeT", b), ("SAb", hh)], writes=[pok])
                    S.op("pe", lambda e, cs=cs, hh=hh: e.matmul(po[64:128, cs], lhsT=qeT[b][:, hh * 128 + 64:hh * 128 + 128],
                                                                rhs=SBb[:, cs], start=False, stop=True),
                         reads=[("qeT", b), ("SBb", hh)], writes=[pok])

            def b2(j):
                u, il = items[j]
                hg = units[u][2]
                p = u % 2
                po, pok = ps_pin(j % 2)
                S.op("act", lambda e: e.activation(out=sq, in_=po[:], func=AF.Square), reads=[pok], writes=["sq"])
                yield
                S.op("dve", lambda e: e.tensor_reduce(out=ssq, in_=v3(sq), axis=AX.X, op=ALU.add), reads=["sq"], writes=["ssq"])
                yield
                S.op("act", lambda e: e.activation(out=rst, in_=ssq, func=AF.Ln, scale=1.0 / 128, bias=EPS), reads=["ssq"], writes=["rst"])
                S.op("act", lambda e: e.activation(out=rst, in_=rst, func=AF.Exp, scale=-0.5), reads=["rst"], writes=["rst"])
                yield
                S.op("dve", lambda e: e.tensor_tensor(out=v3(on), in0=v3(po[:]), in1=rst.unsqueeze(2).broadcast_to([128, 4, 128]),
                                                      op=ALU.mult), reads=[pok, "rst"], writes=["on"])
                yield
                S.op("dve", lambda e: e.tensor_tensor(out=og, in0=on, in1=gs[p][:, il, :], op=ALU.mult),
                     reads=["on", ("gs", p, il)], writes=["og"])
                yield
                pg, pgk = ps()
                for hh in range(4):
                    S.op("pe", lambda e, hh=hh: e.matmul(pg[:, hh * 128:(hh + 1) * 128], lhsT=og[:, hh * 128:(hh + 1) * 128],
                                                         rhs=identb[:], start=True, stop=True), reads=["og", "identb"], writes=[pgk])
                S.op("act", lambda e: e.activation(out=ogT[:, hg * 4:(hg + 1) * 4, il * 128:(il + 1) * 128], in_=v3(pg[:]), func=AF.Copy),
                     reads=[pgk], writes=[("ogT", hg * 4 + hh, il) for hh in range(4)])

            for kind in (0, 3):
                for grp in inproj_groups(0, kind):
                    grp()
            load_k(0)
            if NI > 1:
                load_k(1)
            sched_q = {NS * (u1 - 1): u1 for u1 in range(1, NU)}
            sched_g = {NS * (u1 - 1) + 2: u1 for u1 in range(1, NU)}
            for t in range(NI + 3):
                gens = []
                if 0 <= t - 3 < NI:
                    gens.append(b2(t - 3))
                if 0 <= t - 1 < NI:
                    gens.append(f2(t - 1))
                if t < NI:
                    gens.append(f1(t))
                if 0 <= t - 2 < NI:
                    gens.append(b1(t - 2))
                drive(*gens)
                if t + 2 < NI:
                    load_k(t + 2)
                if t < NI:
                    load_v(t)
                if t in sched_g:
                    for grp in inproj_groups(sched_g[t], 3):
                        grp()
                if t in sched_q:
                    u1 = sched_q[t]
                    if units[u1][0] != units[u1 - 1][0]:
                        rmsnorm_T(0, units[u1][0])
                    for grp in inproj_groups(u1, 0):
                        grp()
                jb = t - 3
                if 0 <= jb < NI:
                    u, il = items[jb]
                    if il == NS - 1 and units[u][2] == 1:
                        ck, s, _ = units[u]
                        out_proj(w_hout, ogT, "ogT", [(i_, ck * NT + s * NS + i_) for i_ in range(NS)])
            rot[0] = NROT

        def hgrn_state_pass():
            phase_barrier("hstate")
            ar_off[0] = 0
            kst4 = [[carve(NS * 512).rearrange("p (i c) -> p i c", c=512) for _ in range(2)] for _ in range(2)]
            vb4 = [[carve(NS * 256, BF16).rearrange("p (i c) -> p i c", c=512) for _ in range(2)] for _ in range(2)]
            logf = [carve(512) for _ in range(2)]
            lhi = [carve(256, BF16) for _ in range(2)]
            llo = [carve(256, BF16) for _ in range(2)]
            eR = [carve(512) for _ in range(2)]
            kd = [[carve(256, BF16), carve(256, BF16)] for _ in range(2)]
            dS = [[carve(8), carve(8)] for _ in range(2)]
            units = [(ck, s, hg) for ck in range(NCK) for s in range(NSUB) for hg in range(2)]
            NU = len(units)

            def inproj_gen(u):
                ck, s, hg = units[u]
                wp = (u // 2) % 2
                kst, vb = kst4[wp], vb4[wp]
                i0 = s * NS
                for kind in (1, 2):
                    bi = hg * 4 + kind
                    wt, wk = wload(w_hin[bi * 128:(bi + 1) * 128, :], 4096, 2048)
                    for il in range(NS):
                        i = i0 + il
                        pt, pk = ps()
                        for k in range(8):
                            S.op("pe", lambda e, pt=pt, k=k, i=i, wt=wt: e.matmul(
                                pt[:], lhsT=hnT[:, k, i * 128:(i + 1) * 128], rhs=wt[:, k * 512:(k + 1) * 512],
                                start=(k == 0), stop=(k == 7)), reads=[wk, ("hn", k, i)], writes=[pk])
                        r0 = (hg * NTQ + ck * NT + i) * 128
                        if kind == 1:
                            S.op("act", lambda e, pt=pt, il=il: e.activation(out=kst[hg][:, il, :], in_=pt[:], func=AF.Sigmoid,
                                                                            scale=-1.0), reads=[pk], writes=[("k", wp, hg, il)])
                            S.op("dve", lambda e, il=il: e.tensor_tensor(out=kst[hg][:, il, :], in0=kst[hg][:, il, :],
                                                                        in1=oml[:, hg * 512:(hg + 1) * 512], op=ALU.mult),
                                 reads=[("k", wp, hg, il), "oml"], writes=[("k", wp, hg, il)])
                            S.dma("sp", ks_d[r0:r0 + 128, :], kst[hg][:, il, :], reads=[("k", wp, hg, il)], writes=[("ksd", hg, il)])
                        else:
                            S.op("act", lambda e, pt=pt, il=il: e.activation(out=vb[hg][:, il, :], in_=pt[:], func=AF.Copy),
                                 reads=[pk], writes=[("vb", wp, hg, il)])
                            S.dma("sp", vs_d[r0:r0 + 128, :], vb[hg][:, il, :], reads=[("vb", wp, hg, il)], writes=[("vsd", hg, il)])
                        yield

            def chain(u):
                ck, s, hg = units[u]
                wp = (u // 2) % 2
                kst, vb = kst4[wp], vb4[wp]
                for i in range(NS):
                    b = i % 2
                    S.op("act", lambda e, i=i: e.activation(out=logf[hg], in_=kst[hg][:, i, :], func=AF.Ln, scale=-1.0, bias=1.0),
                         reads=[("k", wp, hg, i)], writes=[("logf", hg)])
                    S.op("act", lambda e, i=i: e.activation(out=lhi[hg], in_=kst[hg][:, i, :], func=AF.Ln, scale=-1.0, bias=1.0),
                         reads=[("k", wp, hg, i)], writes=[("lhi", hg)])
                    yield
                    S.op("dve", lambda e: e.tensor_tensor(out=llo[hg], in0=logf[hg], in1=lhi[hg], op=ALU.subtract),
                         reads=[("logf", hg), ("lhi", hg)], writes=[("llo", hg)])
                    yield
                    pr, prk = ps()
                    S.op("pe", lambda e, pr=pr: e.matmul(pr[:], lhsT=Ufb, rhs=lhi[hg], start=True, stop=False),
                         reads=[("lhi", hg), "matsb"], writes=[prk])
                    S.op("pe", lambda e, pr=pr: e.matmul(pr[:], lhsT=Ufb, rhs=llo[hg], start=False, stop=True),
                         reads=[("llo", hg), "matsb"], writes=[prk])
                    pd, pdk = ps()
                    for hh in range(4):
                        S.op("pe", lambda e, hh=hh, pd=pd: e.matmul(pd[:, 2 * hh:2 * hh + 2], lhsT=lhi[hg][:, hh * 128:(hh + 1) * 128],
                                                                    rhs=oneb, start=True, stop=False),
                             reads=[("lhi", hg), "matsb"], writes=[pdk])
                        S.op("pe", lambda e, hh=hh, pd=pd: e.matmul(pd[:, 2 * hh:2 * hh + 2], lhsT=llo[hg][:, hh * 128:(hh + 1) * 128],
                                                                    rhs=oneb, start=False, stop=True),
                             reads=[("llo", hg), "matsb"], writes=[pdk])
                    S.op("act", lambda e, pr=pr: e.activation(out=eR[hg], in_=pr[:], func=AF.Exp), reads=[prk], writes=[("eR", hg)])
                    S.op("act", lambda e, pd=pd, b=b: e.activation(out=dS[hg][b], in_=pd[:, 0:8], func=AF.Exp),
                         reads=[pdk], writes=[("dS", hg, b)])
                    yield
                    S.op("dve", lambda e, i=i, b=b: e.tensor_tensor(out=kd[hg][b], in0=kst[hg][:, i, :], in1=eR[hg], op=ALU.mult),
                         reads=[("k", wp, hg, i), ("eR", hg)], writes=[("kd", hg, b)])
                    yield
                    pua, puak = ps()
                    for hh in range(4):
                        cs = slice(hh * 128, (hh + 1) * 128)
                        S.op("pe", lambda e, cs=cs, i=i, b=b, pua=pua: e.matmul(pua[:, cs], lhsT=kd[hg][b][:, cs],
                                                                               rhs=vb[hg][:, i, cs], start=True, stop=True),
                             reads=[("kd", hg, b), ("vb", wp, hg, i)], writes=[puak])
                    for hh in range(4):
                        h = hg * 4 + hh
                        cs = slice(hh * 128, (hh + 1) * 128)
                        Sh = Sst[:, h * 128:(h + 1) * 128]
                        S.op("dve", lambda e, cs=cs, Sh=Sh, hh=hh, b=b, pua=pua: e.scalar_tensor_tensor(
                            out=Sh, in0=Sh, scalar=dS[hg][b][:, 2 * hh:2 * hh + 1], in1=pua[:, cs],
                            op0=ALU.mult, op1=ALU.add), reads=[("S", h), ("dS", hg, b), puak], writes=[("S", h)])
                    S.op("dve", lambda e, b=b: e.tensor_tensor(out=Dt[:, hg * 4:(hg + 1) * 4], in0=Dt[:, hg * 4:(hg + 1) * 4],
                                                               in1=dS[hg][b].rearrange("p (h c) -> p h c", c=2)[:, :, 0], op=ALU.mult),
                         reads=[("dS", hg, b), ("Dt", hg)], writes=[("Dt", hg)])
                    yield

            def chain_seq(g):
                yield from g

            rmsnorm_T(0, 0)
            drive(inproj_gen(0))
            drive(inproj_gen(1))
            for w in range(NU // 2):
                nxt = []
                if 2 * w + 2 < NU:
                    if units[2 * w + 2][0] != units[2 * w][0]:
                        rmsnorm_T(0, units[2 * w + 2][0])

                    def both(w=w):
                        yield from inproj_gen(2 * w + 2)
                        yield from inproj_gen(2 * w + 3)
                    nxt = [both()]
                drive(chain(2 * w), chain(2 * w + 1), *nxt)

        def horner(G, nS, keyG):
            nh = nS // 128
            allS = [("S", h) for h in range(8)]
            Sv = Sst[:, 0:nS].rearrange("p (h d) -> p h d", h=nh)
            S.op("dve", lambda e: e.memset(Sst[:, 0:nS], 0.0), writes=allS)
            for j in range(4):
                mj = msk[:, j:j + 1]
                omj = msk[:, 4 + j:5 + j]
                S.op("dve", lambda e, j=j, mj=mj, omj=omj: e.tensor_scalar(out=dm, in0=G[:, j, nS:nS + 8], scalar1=mj, scalar2=omj,
                                                                        op0=ALU.mult, op1=ALU.add),
                     reads=[keyG, "msk"], writes=["dm"])
                S.op("dve", lambda e: e.tensor_tensor(out=Sv, in0=Sv, in1=dm[:, 0:nh].unsqueeze(2).broadcast_to([128, nh, 128]),
                                                      op=ALU.mult), reads=["dm"] + allS, writes=allS)
                S.op("dve", lambda e, j=j, mj=mj: e.scalar_tensor_tensor(out=Sst[:, 0:nS], in0=G[:, j, 0:nS], scalar=mj, in1=Sst[:, 0:nS],
                                                                        op0=ALU.mult, op1=ALU.add),
                     reads=[keyG, "msk"] + allS, writes=allS)

        def ffn_layer(l, ck):
            phase_barrier("ffn")
            ar_off[0] = 0
            actT = carve(11 * T // 2, BF16).rearrange("p (j t) -> p j t", t=T)
            sg = [carve(512), carve(512)]
            for h in range(2):
                for jj in range(11):
                    j = 11 * h + jj
                    r0 = (l * 22 + j) * 128
                    wt, wk = wload(w_fin[r0:r0 + 128, :], 2048, 2048)
                    for tb in range(NTB):
                        ts_ = slice(tb * TB, (tb + 1) * TB)
                        hk = lambda k: [("hn", k, i) for i in range(tb * TPB, (tb + 1) * TPB)]
                        pg_, pgk = ps()
                        pu_, puk = ps()
                        for k in range(8):
                            S.op("pe", lambda e, pg_=pg_, k=k, ts_=ts_, wt=wt: e.matmul(
                                pg_[:, 0:TB], lhsT=wt[:, k * 256:k * 256 + 128], rhs=hnT[:, k, ts_], start=(k == 0), stop=(k == 7)),
                                reads=[wk] + hk(k), writes=[pgk])
                        for k in range(8):
                            S.op("pe", lambda e, pu_=pu_, k=k, ts_=ts_, wt=wt: e.matmul(
                                pu_[:, 0:TB], lhsT=wt[:, k * 256 + 128:k * 256 + 256], rhs=hnT[:, k, ts_], start=(k == 0), stop=(k == 7)),
                                reads=[wk] + hk(k), writes=[puk])
                        sgt = sg[tb % 2]
                        S.op("act", lambda e, sgt=sgt, pg_=pg_: e.activation(out=sgt[:, 0:TB], in_=pg_[:, 0:TB], func=AF.Silu),
                             reads=[pgk], writes=[("sg", tb % 2)])
                        S.op("dve", lambda e, sgt=sgt, pu_=pu_, jj=jj, ts_=ts_: e.tensor_tensor(
                            out=actT[:, jj, ts_], in0=sgt[:, 0:TB], in1=pu_[:, 0:TB], op=ALU.mult),
                            reads=[("sg", tb % 2), puk], writes=[("actT", jj, tb)])
                for ch in range(2):
                    r0 = ((l * 2 + h) * 2 + ch) * 128
                    wt, wk = wload(w_fout[r0:r0 + 128, :], SLOT, 1408)
                    for i in range(NT):
                        g = ck * NT + i
                        pt, pk = ps()
                        for jj in range(11):
                            S.op("pe", lambda e, pt=pt, jj=jj, i=i, wt=wt: e.matmul(
                                pt[:], lhsT=actT[:, jj, i * 128:(i + 1) * 128], rhs=wt[:, jj * 512:(jj + 1) * 512],
                                start=(jj == 0), stop=(jj == 10)), reads=[wk, ("actT", jj, i // TPB)], writes=[pk])
                        xo = x_sb[:, g, ch * 512:(ch + 1) * 512]
                        S.op("dve", lambda e, xo=xo, pt=pt: e.tensor_tensor(out=xo, in0=xo, in1=pt[:], op=ALU.add),
                             reads=[pk, ("x", g)], writes=[("x", g)])

        lv3 = lv[:].rearrange("p (c v) -> p c v", v=8)

        def lru_sub(ck, s):
            phase_barrier("lru")
            ar_off[0] = 0
            yt = [[carve(TS), carve(TS)] for _ in range(2)]
            u = [[carve(TS + 4), carve(TS + 4)] for _ in range(2)]
            uc = [[carve(TS), carve(TS)] for _ in range(2)]
            ucb = [[carve(TS // 2, BF16), carve(TS // 2, BF16)] for _ in range(2)]
            r = [carve(TS), carve(TS)]
            ig = [carve(TS), carve(TS)]
            a2 = [carve(TS), carve(TS)]
            hs = [carve(TS), carve(TS)]
            zst = [carve(TS // 2, BF16), carve(TS // 2, BF16)]
            hyT = carve(4 * TS, BF16).rearrange("p (c t) -> p c t", t=TS)
            t0 = s * TS
            q0 = ck * T + s * TS
            hkeys = [("hn", k, i) for k in range(8) for i in range(s * NS, (s + 1) * NS)]
            wts = {}

            def load_w(n):
                wt, wk = wload(w_lin[n * 128:(n + 1) * 128, :], 4096, 2048)
                for gi in range(2):
                    S.dma("pool", wt[:, 4096 + gi * 512:4096 + (gi + 1) * 512],
                          w_ax[gi * 128:(gi + 1) * 128, n * 512:(n + 1) * 512], reads=[wk], writes=[wk])
                wts[n] = (wt, wk)

            def phase1(n, q):
                p = n % 2
                wt, wk = wts[n]
                c = 2 * n + q
                yk, uk, uck, ucbk = ("yt", p, q), ("u", p, q), ("uc", p, q), ("ucb", p, q)
                py, pyk = ps()
                for k in range(8):
                    S.op("pe", lambda e, k=k: e.matmul(
                        py[:, 0:TS], lhsT=wt[:, k * 512 + q * 128:k * 512 + q * 128 + 128], rhs=hnT[:, k, t0:t0 + TS],
                        start=(k == 0), stop=(k == 7)), reads=[wk] + [hk for hk in hkeys if hk[1] == k], writes=[pyk])
                S.op("act", lambda e: e.activation(out=yt[p][q], in_=py[:, 0:TS], func=AF.Gelu_apprx_tanh), reads=[pyk], writes=[yk])
                yield
                S.op("dve", lambda e: e.tensor_copy(out=u[p][q][:, 0:3], in_=halo[:, 3 * c:3 * c + 3]),
                     reads=[("halo", c)], writes=[uk])
                pu, puk = ps()
                for k in range(8):
                    S.op("pe", lambda e, k=k: e.matmul(
                        pu[:, 0:TS], lhsT=wt[:, k * 512 + 256 + q * 128:k * 512 + 256 + q * 128 + 128], rhs=hnT[:, k, t0:t0 + TS],
                        start=(k == 0), stop=(k == 7)), reads=[wk] + [hk for hk in hkeys if hk[1] == k], writes=[puk])
                S.op("act", lambda e: e.activation(out=u[p][q][:, 3:3 + TS], in_=pu[:, 0:TS], func=AF.Copy), reads=[puk], writes=[uk])
                yield
                S.op("dve", lambda e: e.tensor_copy(out=halo[:, 3 * c:3 * c + 3], in_=u[p][q][:, TS:TS + 3]),
                     reads=[uk], writes=[("halo", c)])
                S.op("act", lambda e: e.activation(out=uc[p][q], in_=u[p][q][:, 0:TS], func=AF.Identity, scale=lv3[:, c, 0:1],
                                                   bias=lv3[:, c, 4:5]), reads=[uk, "lv"], writes=[uck])
                yield
                for tap in range(1, 4):
                    S.op("dve", lambda e, tap=tap: e.scalar_tensor_tensor(
                        out=uc[p][q], in0=u[p][q][:, tap:tap + TS], scalar=lv3[:, c, tap:tap + 1], in1=uc[p][q],
                        op0=ALU.mult, op1=ALU.add), reads=[uk, uck, "lv"], writes=[uck])
                    yield
                S.op("act", lambda e: e.activation(out=ucb[p][q], in_=uc[p][q], func=AF.Copy), reads=[uck], writes=[ucbk])

            def phase2(n, q):
                p = n % 2
                wt, wk = wts[n]
                d = 2 * n + q
                yk, uck = ("yt", p, q), ("uc", p, q)
                pa, pak = ps()
                px, pxk = ps()
                for gi, pp in ((0, pa), (1, px)):
                    for qq in range(2):
                        o0 = 4096 + gi * 512 + qq * 256 + q * 128
                        S.op("pe", lambda e, pp=pp, qq=qq, o0=o0: e.matmul(
                            pp[:, 0:TS], lhsT=wt[:, o0:o0 + 128], rhs=ucb[p][qq], start=(qq == 0), stop=(qq == 1)),
                            reads=[wk, ("ucb", p, qq)], writes=[pak if gi == 0 else pxk])
                S.op("act", lambda e: e.activation(out=r[q], in_=pa[:, 0:TS], func=AF.Tanh, scale=0.5, bias=scl[:, 16 + d:17 + d]),
                     reads=[pak, "scl"], writes=[("r", q)])
                S.op("act", lambda e: e.activation(out=ig[q], in_=px[:, 0:TS], func=AF.Tanh, scale=0.5, bias=scl[:, 24 + d:25 + d]),
                     reads=[pxk, "scl"], writes=[("ig", q)])
                yield
                S.op("act", lambda e: e.activation(out=a2[q], in_=r[q], func=AF.Exp, scale=scl[:, d:d + 1], bias=scl[:, d:d + 1]),
                     reads=[("r", q), "scl"], writes=[("a2", q)])
                S.op("act", lambda e: e.activation(out=r[q], in_=r[q], func=AF.Exp, scale=scl[:, 8 + d:9 + d], bias=scl[:, 8 + d:9 + d]),
                     reads=[("r", q), "scl"], writes=[("r", q)])
                yield
                S.op("dve", lambda e: e.tensor_scalar(out=a2[q], in0=a2[q], scalar1=0.99999994, scalar2=None, op0=ALU.min),
                     reads=[("a2", q)], writes=[("a2", q)])
                yield
                S.op("act", lambda e: e.activation(out=a2[q], in_=a2[q], func=AF.Ln, scale=-1.0, bias=1.0),
                     reads=[("a2", q)], writes=[("a2", q)])
                S.op("act", lambda e: e.activation(out=a2[q], in_=a2[q], func=AF.Exp, scale=0.5),
                     reads=[("a2", q)], writes=[("a2", q)])
                yield
                S.op("dve", lambda e: e.scalar_tensor_tensor(out=ig[q], in0=ig[q], scalar=1.0, in1=a2[q], op0=ALU.add, op1=ALU.mult),
                     reads=[("a2", q), ("ig", q)], writes=[("ig", q)])
                S.op("dve", lambda e: e.scalar_tensor_tensor(out=ig[q], in0=ig[q], scalar=0.5, in1=uc[p][q], op0=ALU.mult, op1=ALU.mult),
                     reads=[("ig", q), uck], writes=[("ig", q)])
                yield
                S.op("dve", lambda e: e.tensor_tensor_scan(out=hs[q], data0=r[q], data1=ig[q], initial=hloc[:, d:d + 1],
                                                           op0=ALU.mult, op1=ALU.add),
                     reads=[("r", q), ("ig", q), ("hloc", d)], writes=[("hs", q)])
                yield
                S.op("dve", lambda e: e.tensor_copy(out=hloc[:, d:d + 1], in_=hs[q][:, TS - 1:TS]),
                     reads=[("hs", q)], writes=[("hloc", d)])
                S.op("dve", lambda e: e.tensor_tensor(out=hyT[:, d, :], in0=hs[q], in1=yt[p][q], op=ALU.mult),
                     reads=[("hs", q), yk], writes=[("hyT", d, il) for il in range(NS)])
                yield
                S.op("dve", lambda e: e.tensor_tensor_scan(out=hs[q], data0=r[q], data1=ig[q], initial=pcs[:, d:d + 1],
                                                           op0=ALU.mult, op1=ALU.bypass),
                     reads=[("r", q), ("ig", q), ("pcs", d)] + [("hyT", d, il) for il in range(NS)], writes=[("hs", q)])
                yield
                S.op("dve", lambda e: e.tensor_copy(out=pcs[:, d:d + 1], in_=hs[q][:, TS - 1:TS]),
                     reads=[("hs", q)], writes=[("pcs", d)])
                S.op("dve", lambda e: e.tensor_tensor(out=zst[q], in0=hs[q], in1=yt[p][q], op=ALU.mult),
                     reads=[("hs", q), yk], writes=[("zst", q)])
                S.dma("sp", zd[d * 128:(d + 1) * 128, q0:q0 + TS], zst[q], reads=[("zst", q)], writes=[("zd", d, q0)])

            load_w(0)
            drive(phase1(0, 0), phase1(0, 1))
            for n in range(4):
                if n + 1 < 4:
                    load_w(n + 1)
                    drive(phase2(n, 0), phase1(n + 1, 0), phase2(n, 1), phase1(n + 1, 1))
                else:
                    drive(phase2(n, 0), phase2(n, 1))
            out_proj(w_lout, hyT, "hyT", [(il, ck * NT + s * NS + il) for il in range(NS)])

        def allgather(src_sb, ncols, ag_in, ag_out, G, tag, src_keys):
            S.dma("sp", ag_in, src_sb, reads=src_keys, writes=[tag + "_in"])
            S.coll(lambda e: e.collective_compute("AllGather", ALU.bypass, replica_groups=RG, ins=[ag_in], outs=[ag_out]),
                   reads=[tag + "_in"], writes=[tag + "_out"])
            S.dma("sp", G, ag_out.rearrange("(r p) c -> p r c", p=128), reads=[tag + "_out"], writes=[tag + "_G"])

        def final_norm_store(tiles):
            for n_, g in enumerate(tiles):
                b = n_ % 2
                ss = ssq1[:, 4 + 2 * b:5 + 2 * b]
                rs = ssq1[:, 5 + 2 * b:6 + 2 * b]
                S.op("act", lambda e, g=g, ss=ss, b=b: e.activation(out=xsb[b], in_=x_sb[:, g, :], func=AF.Square, accum_out=ss),
                     reads=[("x", g)], writes=[("xs", b), ("fss", b)])
                S.op("act", lambda e, ss=ss, rs=rs: e.activation(out=rs, in_=ss, func=AF.Sqrt, scale=1.0 / D, bias=EPS),
                     reads=[("fss", b)], writes=[("frs", b)])
                S.op("dve", lambda e, rs=rs: e.reciprocal(out=rs, in_=rs), reads=[("frs", b)], writes=[("frs", b)])
                S.op("dve", lambda e, g=g, rs=rs: e.scalar_tensor_tensor(out=x_sb[:, g, :], in0=x_sb[:, g, :], scalar=rs, in1=nfin[:],
                                                                        op0=ALU.mult, op1=ALU.mult),
                     reads=[("x", g), ("frs", b), "nfin"], writes=[("x", g)])
                S.dma("sp", out_d[g * 128:(g + 1) * 128, :], x_sb[:, g, :], reads=[("x", g)], writes=[("out", g)])

        if STAGE >= 2:
            hgrn_state_pass()
            S.barrier()
            ar_off[0] = 0
            G1 = carve(4 * NG1).rearrange("p (r c) -> p r c", c=NG1)
            S.dma("sp", ag1_in, Sst[:], reads=[("S", h) for h in range(8)] + [("Dt", 0), ("Dt", 1)], writes=["ag1_in"])
            S.coll(lambda e: e.collective_compute("AllGather", ALU.bypass, replica_groups=RG, ins=[ag1_in], outs=[ag1_out]),
                   reads=["ag1_in"], writes=["ag1_out"])
            rmsnorm_T(0, 0)
            S.dma("sp", G1, ag1_out.rearrange("(r p) c -> p r c", p=128), reads=["ag1_out"], writes=["ag1_G"])
            horner(G1, 1024, "ag1_G")
            hgrn_full()
        hl = sm[:, 64:88]
        G2 = sm[:, 88:184].rearrange("p (r c) -> p r c", c=24)

        def halo_prepass():
            rmsnorm_T(2, NCK - 1, tiles=[NT - 1])
            for n in range(4):
                wt, wk = wload(w_lin[n * 128:(n + 1) * 128, :], 4096, 2048)
                for q in range(2):
                    c = 2 * n + q
                    pu, puk = ps()
                    for k in range(8):
                        S.op("pe", lambda e, pu=pu, k=k, q=q, wt=wt: e.matmul(
                            pu[:, 0:128], lhsT=wt[:, k * 512 + 256 + q * 128:k * 512 + 256 + q * 128 + 128],
                            rhs=hnT[:, k, (NT - 1) * 128:NT * 128], start=(k == 0), stop=(k == 7)),
                            reads=[wk, ("hn", k, NT - 1)], writes=[puk])
                    S.op("act", lambda e, pu=pu, c=c: e.activation(out=hl[:, 3 * c:3 * c + 3], in_=pu[:, 125:128], func=AF.Copy),
                         reads=[puk], writes=["ag2_src"])
            S.dma("sp", ag2_in, hl, reads=["ag2_src"], writes=["ag2_in"])
            S.coll(lambda e: e.collective_compute("AllGather", ALU.bypass, replica_groups=RG, ins=[ag2_in], outs=[ag2_out]),
                   reads=["ag2_in"], writes=["ag2_out"])

        if STAGE >= 3:
            for ck in reversed(range(NCK)):
                rmsnorm_T(1, ck)
                ffn_layer(0, ck)
                if ck == NCK - 1 and STAGE >= 4:
                    S.barrier()
                    halo_prepass()
        if STAGE >= 4:
            S.barrier()
            S.dma("sp", G2, ag2_out.rearrange("(r p) c -> p r c", p=128), writes=["ag2_G"])
            S.op("dve", lambda e: e.tensor_scalar(out=halo, in0=G2[:, 0, :], scalar1=msk[:, 8:9], scalar2=None, op0=ALU.mult),
                 reads=["ag2_G", "msk"], writes=[("halo", c) for c in range(8)])
            for j in range(1, 4):
                S.op("dve", lambda e, j=j: e.scalar_tensor_tensor(out=halo, in0=G2[:, j, :], scalar=msk[:, 8 + j:9 + j], in1=halo,
                                                                 op0=ALU.mult, op1=ALU.add),
                     reads=["ag2_G", "msk"] + [("halo", c) for c in range(8)], writes=[("halo", c) for c in range(8)])
            for ck in range(NCK):
                rmsnorm_T(2, ck)
                for s in range(NSUB):
                    lru_sub(ck, s)
            S.barrier()
            ar_off[0] = 0
            G3 = carve(4 * 16).rearrange("p (r c) -> p r c", c=16)
            src3 = carve(16)
            Zt = [carve(4 * 512, BF16).rearrange("p (c t) -> p c t", t=512) for _ in range(2)]
            S.op("dve", lambda e: e.tensor_copy(out=src3[:, 0:8], in_=hloc), reads=[("hloc", d) for d in range(8)], writes=["ag3_src"])
            S.op("dve", lambda e: e.tensor_copy(out=src3[:, 8:16], in_=pcs), reads=[("pcs", d) for d in range(8)] + ["ag3_src"],
                 writes=["ag3_src"])
            allgather(src3, 16, ag3_in, ag3_out, G3, "ag3", ["ag3_src"])
            S.op("dve", lambda e: e.memset(hin, 0.0), writes=["hin"])
            for j in range(4):
                mj = msk[:, j:j + 1]
                omj = msk[:, 4 + j:5 + j]
                S.op("dve", lambda e, j=j, mj=mj, omj=omj: e.tensor_scalar(out=dm, in0=G3[:, j, 8:16], scalar1=mj, scalar2=omj,
                                                                        op0=ALU.mult, op1=ALU.add),
                     reads=["ag3_G", "msk"], writes=["dm"])
                S.op("dve", lambda e: e.tensor_tensor(out=hin, in0=hin, in1=dm, op=ALU.mult), reads=["dm", "hin"], writes=["hin"])
                S.op("dve", lambda e, j=j, mj=mj: e.scalar_tensor_tensor(out=hin, in0=G3[:, j, 0:8], scalar=mj, in1=hin,
                                                                        op0=ALU.mult, op1=ALU.add),
                     reads=["ag3_G", "msk", "hin"], writes=["hin"])
            wh = [wload(w_lout[half * 128:(half + 1) * 128, :], 4096, 2048) for half in range(2)]
            GT = min(512, QL)
            zsrc = zd.rearrange("(c p) t -> p c t", p=128)
            for tg in range(QL // GT):
                zt = Zt[tg % 2]
                zk = ("Zt", tg % 2)
                S.dma("sp", zt[:, :, 0:GT], zsrc[:, :, tg * GT:(tg + 1) * GT], writes=[zk])
                for d in range(8):
                    S.op("dve", lambda e, d=d, zt=zt: e.tensor_scalar(out=zt[:, d, 0:GT], in0=zt[:, d, 0:GT], scalar1=hin[:, d:d + 1],
                                                                     scalar2=None, op0=ALU.mult), reads=["hin", zk], writes=[zk])
                for half in range(2):
                    wt, wk = wh[half]
                    for il in range(GT // 128):
                        g = tg * (GT // 128) + il
                        pt, pk = ps()
                        for k in range(8):
                            S.op("pe", lambda e, pt=pt, k=k, il=il, wt=wt, zt=zt: e.matmul(
                                pt[:], lhsT=zt[:, k, il * 128:(il + 1) * 128], rhs=wt[:, k * 512:(k + 1) * 512],
                                start=(k == 0), stop=(k == 7)), reads=[wk, zk], writes=[pk])
                        xo = x_sb[:, g, half * 512:(half + 1) * 512]
                        S.op("dve", lambda e, xo=xo, pt=pt: e.tensor_tensor(out=xo, in0=xo, in1=pt[:], op=ALU.add),
                             reads=[pk, ("x", g)], writes=[("x", g)])
        if STAGE >= 5:
            for ck in range(NCK):
                rmsnorm_T(3, ck)
                ffn_layer(1, ck)
                final_norm_store(range(ck * NT, (ck + 1) * NT))
        else:
            S.barrier()
            final_norm_store(range(NTQ))
        S.barrier()
        S.emit_all()
    return nc


def _tile_k(w, cols):
    sub = w[:, cols]
    C = sub.shape[1]
    return np.ascontiguousarray(sub.reshape(8, 128, C).transpose(1, 0, 2).reshape(128, 8 * C))


def prep_weights(inp):
    f = np.float32
    hw = np.asarray(inp["hgrn_w_in"], f)[0]
    w_hin = np.concatenate([_tile_k(hw, np.arange(kind * 1024 + hg * 512, kind * 1024 + hg * 512 + 512))
                            for hg in range(2) for kind in range(4)], axis=0)
    ho = np.asarray(inp["hgrn_w_out"], f)[0]
    w_hout = np.concatenate([_tile_k(ho, np.arange(h * 512, h * 512 + 512)) for h in range(2)], axis=0)
    fi = np.asarray(inp["ffn_w_in"], f)
    w_fin = np.concatenate([_tile_k(fi[l], np.concatenate([np.arange(j * 128, j * 128 + 128),
                                                           np.arange(DFF + j * 128, DFF + j * 128 + 128)]))
                            for l in range(2) for j in range(22)], axis=0)
    fo = np.asarray(inp["ffn_w_out"], f)
    blocks = []
    for l in range(2):
        for h in range(2):
            for ch in range(2):
                sub = fo[l][h * 1408:(h + 1) * 1408, ch * 512:(ch + 1) * 512]
                blocks.append(sub.reshape(11, 128, 512).transpose(1, 0, 2).reshape(128, SLOT))
    w_fout = np.ascontiguousarray(np.concatenate(blocks, axis=0))
    li = np.asarray(inp["lru_w_in"], f)[0]
    w_lin = np.concatenate([_tile_k(li, np.concatenate([np.arange(n * 256, n * 256 + 256),
                                                        np.arange(1024 + n * 256, 1024 + n * 256 + 256)]))
                            for n in range(4)], axis=0)
    lo = np.asarray(inp["lru_w_out"], f)[0]
    w_lout = np.concatenate([_tile_k(lo, np.arange(h * 512, h * 512 + 512)) for h in range(2)], axis=0)
    ax = []
    for nm in ("lru_wa", "lru_wx"):
        w = np.asarray(inp[nm], f)[0]
        ax.append(w.reshape(8, 128, 256).transpose(1, 0, 2).reshape(128, 2048))
    w_ax = np.ascontiguousarray(np.concatenate(ax, axis=0))
    s = np.arange(128)[:, None]
    t = np.arange(128)[None, :]
    same = (s // 64) == (t // 64)
    ident = np.eye(128, dtype=f)
    Lm = (same & (s <= t)).astype(f)
    Um = (same & (s > t)).astype(f)
    blk = np.stack([(np.arange(128) < 64), (np.arange(128) >= 64)], axis=1).astype(f)
    Uf = (s > t).astype(f)
    ones2 = np.ones((128, 2), f)
    c_mats = np.ascontiguousarray(np.concatenate([ident, Lm, Um, Lm, blk, Uf, ones2], axis=1))
    assert c_mats.shape == (128, 644)
    nm_, nf_ = np.asarray(inp["norm_mix"], f), np.asarray(inp["norm_ffn"], f)
    rows = [nm_[0], nf_[0], nm_[1], nf_[1]]
    c_nrm = np.ascontiguousarray(np.stack([r.reshape(8, 128).T for r in rows], axis=1).reshape(128, 32))
    c_nrmb = np.ascontiguousarray(np.concatenate([np.broadcast_to(r.reshape(1, 1024), (128, 1024)) for r in rows], axis=0))
    vals = [np.asarray(inp["lru_conv_w"], f)[0][i] for i in range(4)] + \
           [np.asarray(inp[k], f)[0] for k in ("lru_conv_b", "lru_ba", "lru_bx", "lru_lambda")]
    c_lv = np.ascontiguousarray(np.stack([v.reshape(8, 128).T for v in vals], axis=2).reshape(128, 64))
    lb = np.asarray(inp["hgrn_lb"], f)
    c_lb = np.ascontiguousarray(np.broadcast_to(lb.reshape(1, 2048), (128, 2048)))
    c_hnw = np.ascontiguousarray(np.broadcast_to(np.asarray(inp["hgrn_norm"], f).reshape(1, 128), (128, 128)))
    c_nfin = np.ascontiguousarray(np.broadcast_to(np.asarray(inp["norm_final"], f).reshape(1, 1024), (128, 1024)))
    return dict(w_hin=w_hin, w_hout=w_hout, w_fin=w_fin, w_fout=w_fout, w_lin=w_lin, w_lout=w_lout, w_ax=w_ax,
                c_mats=c_mats, c_nrm=c_nrm, c_nrmb=c_nrmb, c_lv=c_lv, c_lb=c_lb, c_hnw=c_hnw, c_nfin=c_nfin)


def run(inp, T=1024, NCK=2):
    x = np.asarray(inp["x"], np.float32)
    B, SEQ, _ = x.shape
    QL = T * NCK
    assert SEQ == 4 * QL and B == 2
    nc = build(T, NCK)
    wd = prep_weights(inp)
    in_maps = []
    for c in range(8):
        b, q = c // 4, c % 4
        m = np.zeros(16, np.float32)
        for j in range(4):
            m[j] = 1.0 if j < q else 0.0
            m[4 + j] = 1.0 - m[j]
            m[8 + j] = 1.0 if j == q - 1 else 0.0
        msk = np.ascontiguousarray(np.broadcast_to(m.reshape(1, 16), (128, 16)))
        in_maps.append(dict(wd, x=np.ascontiguousarray(x[b, q * QL:(q + 1) * QL]), c_msk=msk))
    res = run_bass_kernel_spmd(nc, in_maps, core_ids=list(range(8)))
    out = np.empty((B, SEQ, D), np.float32)
    for c in range(8):
        b, q = c // 4, c % 4
        out[b, q * QL:(q + 1) * QL] = res.results[c]["out"]
    return out


def kernel(**inputs):
    return run(inputs, T=1024, NCK=2)
```

```python
import numpy as np
from contextlib import ExitStack
import concourse.bass as bass
import concourse.mybir as mybir
from concourse.bass_utils import run_bass_kernel_spmd

F32 = mybir.dt.float32
BF16 = mybir.dt.bfloat16
AF = mybir.ActivationFunctionType
ALU = mybir.AluOpType
AX = mybir.AxisListType

D = 1024
DFF = 2816
EPS = 1e-6
NSLOT = 3
NPS = 8
STAGE = 5
SLOT = 5632


class Sched:
    ENGS = ("pe", "act", "dve", "pool", "sp")

    def __init__(self, nc, stack, n_dma_lanes=16):
        self.nc = nc
        self.prog = {e: [] for e in self.ENGS}
        self.sem = {}
        for e in ("pe", "act", "dve", "pool"):
            self.sem[e] = stack.enter_context(nc.semaphore("s_" + e))
        self.cnt = {e: 0 for e in self.sem}
        self.known = {e: {} for e in self.ENGS}
        self.lw = {}
        self.rd = {}
        self.lanes = []
        for i in range(n_dma_lanes):
            s = stack.enter_context(nc.semaphore("s_dma%d" % i))
            self.lanes.append([s, 0])
        self.lane_rr = 0
        self.csem = stack.enter_context(nc.semaphore("s_cc"))
        self.ccnt = 0

    def _collect(self, eng, reads, writes):
        deps = {}
        own = id(self.sem[eng]) if eng in self.sem else None

        def add(ev):
            if ev is None:
                return
            s, v, sid = ev
            if eng == "pe" and sid == own:
                return
            if deps.get(sid, (None, 0))[1] < v:
                deps[sid] = (s, v)

        for k in reads:
            add(self.lw.get(k))
        for k in writes:
            add(self.lw.get(k))
            for ev in self.rd.get(k, ()):
                add(ev)
        waits = []
        kn = self.known[eng]
        for sid, (s, v) in deps.items():
            if kn.get(sid, 0) < v:
                kn[sid] = v
                waits.append((s, v))
        return waits

    def _commit(self, ev, reads, writes):
        for k in writes:
            self.lw[k] = ev
            self.rd[k] = []
        for k in reads:
            self.rd.setdefault(k, []).append(ev)

    def op(self, eng, fn, reads=(), writes=()):
        reads = tuple(reads)
        writes = tuple(writes)
        waits = self._collect(eng, reads, writes)
        self.cnt[eng] += 1
        n = self.cnt[eng]
        sem = self.sem[eng]

        def emit(e, waits=waits, fn=fn, sem=sem):
            for (s, v) in waits:
                e.wait_ge(s, v)
            fn(e).then_inc(sem, 1)

        self.prog[eng].append(emit)
        self._commit((sem, n, id(sem)), reads, writes)

    def dma(self, q, out, in_, reads=(), writes=()):
        reads = tuple(reads)
        writes = tuple(writes)
        waits = self._collect(q, reads, writes)
        lane = self.lanes[self.lane_rr]
        self.lane_rr = (self.lane_rr + 1) % len(self.lanes)
        sid = id(lane[0])
        if lane[1] > 0 and self.known[q].get(sid, 0) < lane[1]:
            self.known[q][sid] = lane[1]
            waits.append((lane[0], lane[1]))
        lane[1] += 16
        sem, val = lane[0], lane[1]

        def emit(e, waits=waits, sem=sem, out=out, in_=in_):
            for (s, v) in waits:
                e.wait_ge(s, v)
            e.dma_start(out=out, in_=in_).then_inc(sem, 16)

        self.prog[q].append(emit)
        self._commit((sem, val, sid), reads, writes)

    def coll(self, fn, reads=(), writes=()):
        reads = tuple(reads)
        writes = tuple(writes)
        waits = self._collect("pool", reads, writes)
        self.ccnt += 1
        sem, val = self.csem, self.ccnt

        def emit(e, waits=waits, sem=sem, fn=fn):
            for (s, v) in waits:
                e.wait_ge(s, v)
            fn(e).then_inc(sem, 1)

        self.prog["pool"].append(emit)
        self._commit((sem, val, id(sem)), reads, writes)

    def barrier(self):
        self.nbar = getattr(self, "nbar", 0) + 1
        targets = [(self.sem[e], self.cnt[e]) for e in self.sem if self.cnt[e] > 0]
        targets += [(l[0], l[1]) for l in self.lanes if l[1] > 0]
        for eng in self.ENGS:
            if eng == "pool":
                continue
            waits = []
            kn = self.known[eng]
            for (s, v) in targets:
                if kn.get(id(s), 0) < v:
                    kn[id(s)] = v
                    waits.append((s, v))

            def emit(e, waits=waits):
                for (s, v) in waits:
                    e.wait_ge(s, v)

            self.prog[eng].append(emit)
        keep = lambda k: (isinstance(k, tuple) and k and k[0] == "ring") or (isinstance(k, str) and k.startswith("ag"))
        self.lw = {k: v for k, v in self.lw.items() if keep(k)}
        self.rd = {k: v for k, v in self.rd.items() if keep(k)}

    def emit_all(self):
        nc = self.nc
        with nc.Block() as block:
            @block.tensor
            def _(e):
                for f in self.prog["pe"]:
                    f(e)

            @block.scalar
            def _(e):
                for f in self.prog["act"]:
                    f(e)

            @block.vector
            def _(e):
                for f in self.prog["dve"]:
                    f(e)

            @block.gpsimd
            def _(e):
                for f in self.prog["pool"]:
                    f(e)

            @block.sync
            def _(e):
                for f in self.prog["sp"]:
                    f(e)


def build(T, NCK):
    QL = T * NCK
    NT = T // 128
    NTQ = QL // 128
    TB = min(512, T)
    NTB = T // TB
    TPB = TB // 128
    TS = min(512, T)
    NS = TS // 128
    NSUB = T // TS
    nc = bass.Bass("TRN2", target_bir_lowering=False)

    def din(name, shape):
        return nc.dram_tensor(name, list(shape), F32, kind="ExternalInput").ap()

    x_d = din("x", [QL, D])
    out_d = nc.dram_tensor("out", [QL, D], F32, kind="ExternalOutput").ap()
    w_hin = din("w_hin", [8 * 128, 4096])
    w_hout = din("w_hout", [2 * 128, 4096])
    w_fin = din("w_fin", [2 * 22 * 128, 2048])
    w_fout = din("w_fout", [2 * 2 * 2 * 128, SLOT])
    w_lin = din("w_lin", [4 * 128, 4096])
    w_lout = din("w_lout", [2 * 128, 4096])
    w_ax = din("w_ax", [2 * 128, 2048])
    c_mats = din("c_mats", [128, 5 * 128 + 4])
    c_nrm = din("c_nrm", [128, 32])
    c_lv = din("c_lv", [128, 64])
    c_lb = din("c_lb", [128, 2048])
    c_hnw = din("c_hnw", [128, 128])
    c_nfin = din("c_nfin", [128, 1024])
    c_msk = din("c_msk", [128, 16])
    c_nrmb = din("c_nrmb", [4 * 128, 1024])

    def dint(name, shape):
        return nc.dram_tensor(name, list(shape), F32, kind="Internal").ap()

    NG1 = 8 * 128 + 8
    ag1_in = dint("ag1_in", [128, NG1])
    ag1_out = dint("ag1_out", [4 * 128, NG1])
    ag2_in = dint("ag2_in", [128, 24])
    ag2_out = dint("ag2_out", [4 * 128, 24])
    ag3_in = dint("ag3_in", [128, 16])
    ag3_out = dint("ag3_out", [4 * 128, 16])
    zd = nc.dram_tensor("zd", [8 * 128, QL], BF16, kind="Internal").ap()
    ks_d = nc.dram_tensor("ks_d", [2 * NTQ * 128, 512], F32, kind="Internal").ap()
    vs_d = nc.dram_tensor("vs_d", [2 * NTQ * 128, 512], BF16, kind="Internal").ap()
    RG = [[0, 1, 2, 3], [4, 5, 6, 7]]

    with ExitStack() as st:
        S = Sched(nc, st)

        def sb(name, shape, dt=F32):
            return st.enter_context(nc.sbuf_tensor(name, list(shape), dt))

        x_sb = sb("x_sb", [128, NTQ, D])
        hnT = sb("hnT", [128, 8, T], BF16)
        ring = [sb("ring%d" % i, [128, SLOT], BF16) for i in range(NSLOT)]
        ARENA = 17400
        arena = sb("arena", [128, ARENA])
        narena = sb("narena", [128, 2 * D + 8])
        mats = sb("mats", [128, 5 * 128 + 4])
        lv = sb("lv", [128, 64])
        scl = sb("scl", [128, 32])
        oml = sb("oml", [128, D])
        hnw = sb("hnw", [128, 128])
        nfin = sb("nfin", [128, D])
        Sst = sb("Sst", [128, NG1])
        msk = sb("msk", [128, 16])
        identb = sb("identb", [128, 128], BF16)
        matsb = sb("matsb", [128, 388], BF16)
        sm = sb("sm", [128, 192])
        pst = [st.enter_context(nc.psum_tensor("ps%d" % i, [128, 512], F32)) for i in range(NPS)]

        ident = mats[:, 0:128]
        Lm = mats[:, 128:256]
        Um = mats[:, 256:384]
        maskT = mats[:, 384:512]
        blk = mats[:, 512:514]
        Lmb = matsb[:, 0:128]
        Umb = matsb[:, 128:256]
        blkb = matsb[:, 256:258]
        Ufb = matsb[:, 258:386]
        oneb = matsb[:, 386:388]
        Dt = Sst[:, 1024:1032]
        hloc = sm[:, 0:8]
        pcs = sm[:, 8:16]
        hin = sm[:, 16:24]
        halo = sm[:, 24:48]
        dm = sm[:, 48:56]

        ps_i = [0]

        NROT = NPS - 2
        rot = [NROT]

        def ps():
            i = ps_i[0] % rot[0]
            ps_i[0] += 1
            return pst[i], ("ps", i)

        def ps_pin(j):
            return pst[NROT + j], ("ps", NROT + j)

        def ps_pin_u(j):
            return pst[NPS - 4 + j], ("ps", NPS - 4 + j)

        ring_i = [0]

        def wload(src, nelem, inner):
            i = ring_i[0] % NSLOT
            ring_i[0] += 1
            key = ("ring", i)
            dst = ring[i][:, 0:nelem].rearrange("p (a b) -> p a b", b=inner)
            S.dma("pool", dst, src.rearrange("p (a b) -> p a b", b=inner), writes=[key])
            return ring[i], key

        ar_off = [0]
        arena_tag = [None]

        _raw_barrier = S.barrier

        def _tagged_barrier():
            _raw_barrier()
            arena_tag[0] = None

        S.barrier = _tagged_barrier

        def phase_barrier(tag):
            if arena_tag[0] != tag:
                S.barrier()
                arena_tag[0] = tag

        def carve(nwords, dt=F32):
            o = ar_off[0]
            ar_off[0] += nwords
            assert ar_off[0] <= ARENA, ar_off[0]
            a = arena[:, o:o + nwords]
            return a.bitcast(BF16) if dt == BF16 else a

        S.dma("sp", mats[:], c_mats, writes=["mats"])
        S.dma("sp", lv[:], c_lv, writes=["lv"])
        S.dma("sp", hnw[:], c_hnw, writes=["hnw"])
        S.dma("sp", nfin[:], c_nfin, writes=["nfin"])
        S.dma("sp", msk[:], c_msk, writes=["msk"])
        S.dma("sp", arena[:, 0:2048], c_lb, writes=["lbtmp"])
        for g in range(NTQ):
            S.dma("sp", x_sb[:, g, :], x_d[g * 128:(g + 1) * 128, :], writes=[("x", g)])
        S.op("dve", lambda e: e.tensor_tensor(out=oml[:], in0=arena[:, 1024:2048], in1=arena[:, 0:1024],
                                              op=ALU.subtract), reads=["lbtmp"], writes=["oml"])
        S.op("act", lambda e: e.activation(out=oml[:], in_=oml[:], func=AF.Sigmoid), reads=["oml"], writes=["oml"])
        lam = lv[:].rearrange("p (c v) -> p c v", v=8)[:, :, 7]
        S.op("act", lambda e: e.activation(out=scl[:, 0:8], in_=lam, func=AF.Exp, scale=-1.0), reads=["lv"], writes=["scl"])
        S.op("act", lambda e: e.activation(out=scl[:, 0:8], in_=scl[:, 0:8], func=AF.Ln, bias=1.0), reads=["scl"], writes=["scl"])
        S.op("dve", lambda e: e.tensor_scalar(out=scl[:, 8:16], in0=scl[:, 0:8], scalar1=-4.0, scalar2=None, op0=ALU.mult),
             reads=["scl"], writes=["scl"])
        S.op("dve", lambda e: e.tensor_scalar(out=scl[:, 0:8], in0=scl[:, 0:8], scalar1=-8.0, scalar2=None, op0=ALU.mult),
             reads=["scl"], writes=["scl"])
        lvv = lv[:].rearrange("p (c v) -> p c v", v=8)
        S.op("dve", lambda e: e.tensor_scalar(out=scl[:, 16:24], in0=lvv[:, :, 5], scalar1=0.5, scalar2=None, op0=ALU.mult),
             reads=["lv", "scl"], writes=["scl"])
        S.op("dve", lambda e: e.tensor_scalar(out=scl[:, 24:32], in0=lvv[:, :, 6], scalar1=0.5, scalar2=None, op0=ALU.mult),
             reads=["lv", "scl"], writes=["scl"])
        S.op("dve", lambda e: e.memset(Sst[:, 0:1024], 0.0), writes=[("S", h) for h in range(8)])
        S.op("dve", lambda e: e.memset(Dt, 1.0), writes=["Dt"])
        S.op("dve", lambda e: e.tensor_copy(out=identb[:], in_=ident), reads=["mats"], writes=["identb"])
        S.op("dve", lambda e: e.tensor_copy(out=matsb[:, 0:256], in_=mats[:, 128:384]), reads=["mats"], writes=["matsb"])
        S.op("dve", lambda e: e.tensor_copy(out=matsb[:, 256:258], in_=blk), reads=["mats"], writes=["matsb"])
        S.op("dve", lambda e: e.tensor_copy(out=matsb[:, 258:388], in_=mats[:, 514:644]), reads=["mats"], writes=["matsb"])
        S.op("dve", lambda e: e.memset(sm[:], 0.0), writes=["sm"])
        S.op("dve", lambda e: e.memset(pcs, 1.0), reads=["sm"], writes=["sm"])
        S.barrier()

        wb = narena[:, 0:D]
        xsb = [narena[:, D:D + 512].bitcast(BF16), narena[:, D + 512:D + 1024].bitcast(BF16)]
        ssq1 = narena[:, 2 * D:2 * D + 8]

        def rmsnorm_T(ni, ck, tiles=None):
            S.dma("sp", wb, c_nrmb[ni * 128:(ni + 1) * 128, :], writes=["wb"])
            tl = list(range(NT) if tiles is None else tiles)
            pend = {}

            def stage_a(i, b):
                g = ck * NT + i
                ss = ssq1[:, 2 * b:2 * b + 1]
                rs = ssq1[:, 2 * b + 1:2 * b + 2]
                S.op("act", lambda e: e.activation(out=xsb[b], in_=x_sb[:, g, :], func=AF.Square, accum_out=ss),
                     reads=[("x", g)], writes=[("xs", b), ("ss", b)])
                S.op("act", lambda e: e.activation(out=rs, in_=ss, func=AF.Sqrt, scale=1.0 / D, bias=EPS),
                     reads=[("ss", b)], writes=[("rs", b)])
                S.op("dve", lambda e: e.reciprocal(out=rs, in_=rs), reads=[("rs", b)], writes=[("rs", b)])
                S.op("dve", lambda e: e.scalar_tensor_tensor(out=xsb[b], in0=x_sb[:, g, :], scalar=rs, in1=wb,
                                                             op0=ALU.mult, op1=ALU.mult),
                     reads=[("x", g), ("rs", b), "wb"], writes=[("xs", b)])
                banks = []
                for kb in range(2):
                    pt, pk = ps()
                    for kk in range(4):
                        k = kb * 4 + kk
                        S.op("pe", lambda e, pt=pt, kk=kk, k=k: e.matmul(pt[:, kk * 128:(kk + 1) * 128],
                                                                         lhsT=xsb[b][:, k * 128:(k + 1) * 128], rhs=identb[:],
                                                                         start=True, stop=True),
                             reads=[("xs", b), "identb"], writes=[pk])
                    banks.append((pt, pk))
                pend[i] = banks

            def stage_b(i):
                for kb, (pt, pk) in enumerate(pend.pop(i)):
                    o = hnT[:, kb * 4:(kb + 1) * 4, i * 128:(i + 1) * 128]
                    src = pt[:].rearrange("p (k t) -> p k t", k=4)
                    wr = [("hn", kb * 4 + kk, i) for kk in range(4)]
                    if kb == 0:
                        S.op("act", lambda e, o=o, src=src: e.activation(out=o, in_=src, func=AF.Copy), reads=[pk], writes=wr)
                    else:
                        S.op("dve", lambda e, o=o, src=src: e.tensor_copy(out=o, in_=src), reads=[pk], writes=wr)

            if rot[0] < 4:
                for n_, i in enumerate(tl):
                    stage_a(i, n_ % 2)
                    stage_b(i)
                return
            stage_a(tl[0], 0)
            for n_, i in enumerate(tl):
                if n_ + 1 < len(tl):
                    stage_a(tl[n_ + 1], (n_ + 1) % 2)
                stage_b(i)

        def out_proj(w_d, srcT, srckey, pairs):
            for half in range(2):
                wt, wk = wload(w_d[half * 128:(half + 1) * 128, :], 4096, 2048)
                for (si, g) in pairs:
                    pt, pk = ps()
                    for k in range(8):
                        S.op("pe", lambda e, pt=pt, k=k, si=si, wt=wt: e.matmul(
                            pt[:], lhsT=srcT[:, k, si * 128:(si + 1) * 128], rhs=wt[:, k * 512:(k + 1) * 512],
                            start=(k == 0), stop=(k == 7)), reads=[wk, (srckey, k, si)], writes=[pk])
                    xo = x_sb[:, g, half * 512:(half + 1) * 512]
                    S.op("dve", lambda e, xo=xo, pt=pt: e.tensor_tensor(out=xo, in0=xo, in1=pt[:], op=ALU.add),
                         reads=[pk, ("x", g)], writes=[("x", g)])

        def drive(*gens):
            gens = [g for g in gens if g is not None]
            while gens:
                for g in list(gens):
                    try:
                        next(g)
                    except StopIteration:
                        gens.remove(g)

        def v3(a):
            return a.rearrange("p (h d) -> p h d", h=4)

        def hgrn_sub(ck, s, state_only=False):
            phase_barrier("hgrn")
            ar_off[0] = 0
            kst = carve(NS * 512).rearrange("p (i c) -> p i c", c=512)
            qs = carve(NS * 256, BF16).rearrange("p (i c) -> p i c", c=512)
            vb = carve(NS * 256, BF16).rearrange("p (i c) -> p i c", c=512)
            gs = carve(NS * 256, BF16).rearrange("p (i c) -> p i c", c=512)
            ogT = carve(4 * TS, BF16).rearrange("p (h t) -> p h t", t=TS)
            logf = carve(512)
            lhi = carve(256, BF16)
            llo = carve(256, BF16)
            eb = [carve(512), carve(512)]
            enb = [carve(512), carve(512)]
            eR = [carve(512), carve(512)]
            qe = carve(256, BF16)
            ke = carve(256, BF16)
            keT = carve(256, BF16)
            kd = [carve(256, BF16), carve(256, BF16)]
            qeT = [carve(256, BF16), carve(256, BF16)]
            scT = [carve(256, BF16), carve(256, BF16)]
            dS = [carve(8), carve(8), carve(8)]
            sq = carve(512)
            on = carve(512)
            og = carve(256, BF16)
            SBf = carve(512)
            SAb = carve(256, BF16)
            SBb = carve(256, BF16)
            ssq = carve(4)
            rst = carve(4)
            i0 = s * NS
            for hg in range(2):
                for il in range(NS):
                    r0 = (hg * NTQ + ck * NT + i0 + il) * 128
                    S.dma("sp", kst[:, il, :], ks_d[r0:r0 + 128, :], writes=[("k", il)])
                    S.dma("sp", vb[:, il, :], vs_d[r0:r0 + 128, :], writes=[("vb", il)])
                for kind in (0, 3):
                    bi = hg * 4 + kind
                    wt, wk = wload(w_hin[bi * 128:(bi + 1) * 128, :], 4096, 2048)
                    for il in range(NS):
                        i = i0 + il
                        pt, pk = ps()
                        for k in range(8):
                            S.op("pe", lambda e, pt=pt, k=k, i=i, wt=wt: e.matmul(
                                pt[:], lhsT=hnT[:, k, i * 128:(i + 1) * 128], rhs=wt[:, k * 512:(k + 1) * 512],
                                start=(k == 0), stop=(k == 7)), reads=[wk, ("hn", k, i)], writes=[pk])
                        if kind == 0:
                            S.op("act", lambda e, pt=pt, il=il: e.activation(out=qs[:, il, :], in_=pt[:], func=AF.Silu),
                                 reads=[pk], writes=[("qs", il)])
                        elif kind == 1:
                            S.op("act", lambda e, pt=pt, il=il: e.activation(out=kst[:, il, :], in_=pt[:], func=AF.Sigmoid, scale=-1.0),
                                 reads=[pk], writes=[("k", il)])
                            S.op("dve", lambda e, il=il, hg=hg: e.tensor_tensor(out=kst[:, il, :], in0=kst[:, il, :],
                                                                               in1=oml[:, hg * 512:(hg + 1) * 512], op=ALU.mult),
                                 reads=[("k", il), "oml"], writes=[("k", il)])
                        elif kind == 2:
                            S.op("dve", lambda e, pt=pt, il=il: e.tensor_copy(out=vb[:, il, :], in_=pt[:]),
                                 reads=[pk], writes=[("vb", il)])
                        else:
                            S.op("act", lambda e, pt=pt, il=il: e.activation(out=gs[:, il, :], in_=pt[:], func=AF.Silu),
                                 reads=[pk], writes=[("gs", il)])

                def f1(i, hg=hg):
                    b = i % 2
                    b3 = i % 3
                    S.op("act", lambda e: e.activation(out=logf, in_=kst[:, i, :], func=AF.Ln, scale=-1.0, bias=1.0),
                         reads=[("k", i)], writes=["logf"])
                    S.op("act", lambda e: e.activation(out=lhi, in_=kst[:, i, :], func=AF.Ln, scale=-1.0, bias=1.0),
                         reads=[("k", i)], writes=["lhi"])
                    yield
                    S.op("dve", lambda e: e.tensor_tensor(out=llo, in0=logf, in1=lhi, op=ALU.subtract),
                         reads=["logf", "lhi"], writes=["llo"])
                    yield
                    pb, pbk = ps()
                    S.op("pe", lambda e: e.matmul(pb[:], lhsT=Lmb, rhs=lhi, start=True, stop=False), reads=["lhi", "matsb"], writes=[pbk])
                    S.op("pe", lambda e: e.matmul(pb[:], lhsT=Lmb, rhs=llo, start=False, stop=True), reads=["llo", "matsb"], writes=[pbk])
                    pr, prk = ps()
                    S.op("pe", lambda e: e.matmul(pr[:], lhsT=Umb, rhs=lhi, start=True, stop=False), reads=["lhi", "matsb"], writes=[prk])
                    S.op("pe", lambda e: e.matmul(pr[:], lhsT=Umb, rhs=llo, start=False, stop=True), reads=["llo", "matsb"], writes=[prk])
                    pd, pdk = ps()
                    for hh in range(4):
                        S.op("pe", lambda e, hh=hh: e.matmul(pd[:, 2 * hh:2 * hh + 2], lhsT=lhi[:, hh * 128:(hh + 1) * 128],
                                                             rhs=blkb, start=True, stop=False), reads=["lhi", "matsb"], writes=[pdk])
                        S.op("pe", lambda e, hh=hh: e.matmul(pd[:, 2 * hh:2 * hh + 2], lhsT=llo[:, hh * 128:(hh + 1) * 128],
                                                             rhs=blkb, start=False, stop=True), reads=["llo", "matsb"], writes=[pdk])
                    S.op("act", lambda e: e.activation(out=eb[b], in_=pb[:], func=AF.Exp), reads=[pbk], writes=[("eb", b)])
                    S.op("act", lambda e: e.activation(out=enb[b], in_=pb[:], func=AF.Exp, scale=-1.0), reads=[pbk], writes=[("enb", b)])
                    S.op("act", lambda e: e.activation(out=eR[b], in_=pr[:], func=AF.Exp), reads=[prk], writes=[("eR", b)])
                    S.op("act", lambda e: e.activation(out=dS[b3], in_=pd[:, 0:8], func=AF.Exp), reads=[pdk], writes=[("dS", b3)])

                def f2(i, hg=hg):
                    b = i % 2
                    S.op("dve", lambda e: e.tensor_tensor(out=kd[b], in0=kst[:, i, :], in1=eR[b], op=ALU.mult),
                         reads=[("k", i), ("eR", b)], writes=[("kd", b)])
                    S.op("dve", lambda e: e.tensor_tensor(out=qe, in0=qs[:, i, :], in1=eb[b], op=ALU.mult),
                         reads=[("qs", i), ("eb", b)], writes=["qe"])
                    yield
                    S.op("dve", lambda e: e.tensor_tensor(out=ke, in0=kst[:, i, :], in1=enb[b], op=ALU.mult),
                         reads=[("k", i), ("enb", b)], writes=["ke"])
                    pq, pqk = ps()
                    for hh in range(4):
                        S.op("pe", lambda e, hh=hh: e.matmul(pq[:, hh * 128:(hh + 1) * 128], lhsT=qe[:, hh * 128:(hh + 1) * 128],
                                                             rhs=identb[:], start=True, stop=True), reads=["qe", "identb"], writes=[pqk])
                    S.op("act", lambda e: e.activation(out=qeT[b], in_=pq[:], func=AF.Copy), reads=[pqk], writes=[("qeT", b)])
                    yield
                    pk_, pkk = ps()
                    for hh in range(4):
                        S.op("pe", lambda e, hh=hh: e.matmul(pk_[:, hh * 128:(hh + 1) * 128], lhsT=ke[:, hh * 128:(hh + 1) * 128],
                                                             rhs=identb[:], start=True, stop=True), reads=["ke", "identb"], writes=[pkk])
                    S.op("dve", lambda e: e.tensor_copy(out=keT, in_=pk_[:]), reads=[pkk], writes=["keT"])
                    yield
                    psc, psck = ps()
                    for hh in range(4):
                        S.op("pe", lambda e, hh=hh: e.matmul(psc[:, hh * 128:(hh + 1) * 128], lhsT=keT[:, hh * 128:(hh + 1) * 128],
                                                             rhs=qeT[b][:, hh * 128:(hh + 1) * 128], start=True, stop=True),
                             reads=["keT", ("qeT", b)], writes=[psck])
                    S.op("dve", lambda e: e.tensor_tensor(out=v3(scT[b]), in0=v3(psc[:]),
                                                          in1=maskT.unsqueeze(1).broadcast_to([128, 4, 128]), op=ALU.mult),
                         reads=[psck, "mats"], writes=[("scT", b)])

                def b1(i, hg=hg):
                    b = i % 2
                    b3 = i % 3
                    pua, puak = ps()
                    pub, pubk = ps()
                    for hh in range(4):
                        cs = slice(hh * 128, (hh + 1) * 128)
                        S.op("pe", lambda e, cs=cs: e.matmul(pua[:, cs], lhsT=kd[b][0:64, cs], rhs=vb[0:64, i, cs], start=True, stop=True),
                             reads=[("kd", b), ("vb", i)], writes=[puak])
                        S.op("pe", lambda e, cs=cs: e.matmul(pub[:, cs], lhsT=kd[b][64:128, cs], rhs=vb[64:128, i, cs], start=True, stop=True),
                             reads=[("kd", b), ("vb", i)], writes=[pubk])
                    hkeys4 = [("S", hg * 4 + hh) for hh in range(4)]
                    S.op("act", lambda e: e.activation(out=SAb, in_=Sst[:, hg * 512:(hg + 1) * 512], func=AF.Copy),
                         reads=hkeys4, writes=[("SAb", hh) for hh in range(4)])
                    for hh in range(4):
                        h = hg * 4 + hh
                        cs = slice(hh * 128, (hh + 1) * 128)
                        Sh = Sst[:, h * 128:(h + 1) * 128]
                        S.op("dve", lambda e, cs=cs, Sh=Sh, hh=hh: e.scalar_tensor_tensor(
                            out=SBf[:, cs], in0=Sh, scalar=dS[b3][:, 2 * hh:2 * hh + 1], in1=pua[:, cs], op0=ALU.mult, op1=ALU.add),
                            reads=[("S", h), ("dS", b3), puak, ("SAb", hh)], writes=[("SBf", hh)])
                        S.op("dve", lambda e, cs=cs, Sh=Sh, hh=hh: e.scalar_tensor_tensor(
                            out=Sh, in0=SBf[:, cs], scalar=dS[b3][:, 2 * hh + 1:2 * hh + 2], in1=pub[:, cs], op0=ALU.mult, op1=ALU.add),
                            reads=[("SBf", hh), ("dS", b3), pubk], writes=[("S", h)])
                    S.op("act", lambda e: e.activation(out=SBb, in_=SBf, func=AF.Copy),
                         reads=[("SBf", hh) for hh in range(4)], writes=[("SBb", hh) for hh in range(4)])
                    yield
                    po, pok = ps_pin(b)
                    for hh in range(4):
                        cs = slice(hh * 128, (hh + 1) * 128)
                        S.op("pe", lambda e, cs=cs: e.matmul(po[:, cs], lhsT=scT[b][:, cs], rhs=vb[:, i, cs], start=True, stop=False),
                             reads=[("scT", b), ("vb", i)], writes=[pok])
                        S.op("pe", lambda e, cs=cs, hh=hh: e.matmul(po[0:64, cs], lhsT=qeT[b][:, hh * 128:hh * 128 + 64],
                                                                    rhs=SAb[:, cs], start=False, stop=False),
                             reads=[("qeT", b), ("SAb", hh)], writes=[pok])
                        S.op("pe", lambda e, cs=cs, hh=hh: e.matmul(po[64:128, cs], lhsT=qeT[b][:, hh * 128 + 64:hh * 128 + 128],
                                                                    rhs=SBb[:, cs], start=False, stop=True),
                             reads=[("qeT", b), ("SBb", hh)], writes=[pok])

                def b2(i, hg=hg):
                    po, pok = ps_pin(i % 2)
                    S.op("act", lambda e: e.activation(out=sq, in_=po[:], func=AF.Square), reads=[pok], writes=["sq"])
                    yield
                    S.op("dve", lambda e: e.tensor_reduce(out=ssq, in_=v3(sq), axis=AX.X, op=ALU.add), reads=["sq"], writes=["ssq"])
                    yield
                    S.op("act", lambda e: e.activation(out=rst, in_=ssq, func=AF.Ln, scale=1.0 / 128, bias=EPS),
                         reads=["ssq"], writes=["rst"])
                    S.op("act", lambda e: e.activation(out=rst, in_=rst, func=AF.Exp, scale=-0.5), reads=["rst"], writes=["rst"])
                    yield
                    S.op("dve", lambda e: e.tensor_tensor(out=v3(on), in0=v3(po[:]),
                                                          in1=rst.unsqueeze(2).broadcast_to([128, 4, 128]), op=ALU.mult),
                         reads=[pok, "rst"], writes=["on"])
                    yield
                    S.op("dve", lambda e: e.tensor_tensor(out=on, in0=on, in1=gs[:, i, :], op=ALU.mult),
                         reads=["on", ("gs", i)], writes=["on"])
                    yield
                    S.op("dve", lambda e: e.tensor_tensor(out=v3(og), in0=v3(on), in1=hnw[:].unsqueeze(1).broadcast_to([128, 4, 128]),
                                                          op=ALU.mult), reads=["on", "hnw"], writes=["og"])
                    yield
                    pg, pgk = ps()
                    for hh in range(4):
                        S.op("pe", lambda e, hh=hh: e.matmul(pg[:, hh * 128:(hh + 1) * 128], lhsT=og[:, hh * 128:(hh + 1) * 128],
                                                             rhs=identb[:], start=True, stop=True), reads=["og", "identb"], writes=[pgk])
                    S.op("act", lambda e: e.activation(out=ogT[:, hg * 4:(hg + 1) * 4, i * 128:(i + 1) * 128],
                                                       in_=v3(pg[:]), func=AF.Copy),
                         reads=[pgk], writes=[("ogT", hg * 4 + hh, i) for hh in range(4)])

                for step in range(NS + 3):
                    gens = []
                    if step < NS:
                        gens.append(f1(step))
                    if 0 <= step - 1 < NS:
                        gens.append(f2(step - 1))
                    if 0 <= step - 2 < NS:
                        gens.append(b1(step - 2))
                    if 0 <= step - 3 < NS:
                        gens.append(b2(step - 3))
                    drive(*gens)
            out_proj(w_hout, ogT, "ogT", [(il, ck * NT + i0 + il) for il in range(NS)])

        def hgrn_full():
            phase_barrier("hgrn")
            ar_off[0] = 0
            rot[0] = NPS - 4
            qs = [carve(NS * 256, BF16).rearrange("p (i c) -> p i c", c=512) for _ in range(2)]
            gs = [carve(NS * 256, BF16).rearrange("p (i c) -> p i c", c=512) for _ in range(2)]
            kt = [carve(512) for _ in range(3)]
            vt = [carve(256, BF16) for _ in range(3)]
            ogT = carve(4 * TS, BF16).rearrange("p (h t) -> p h t", t=TS)
            logf = carve(512)
            lhi = carve(256, BF16)
            llo = carve(256, BF16)
            eb = [carve(512), carve(512)]
            enb = [carve(512), carve(512)]
            eR = [carve(512), carve(512)]
            qe = carve(256, BF16)
            ke = carve(256, BF16)
            keT = carve(256, BF16)
            kd = [carve(256, BF16), carve(256, BF16)]
            qeT = [carve(256, BF16), carve(256, BF16)]
            scT = [carve(256, BF16), carve(256, BF16)]
            dS = [carve(8), carve(8), carve(8)]
            sq = carve(512)
            on = carve(512)
            og = carve(256, BF16)
            SBf = carve(512)
            SAb = carve(256, BF16)
            SBb = carve(256, BF16)
            ssq = carve(4)
            rst = carve(4)
            units = [(ck, s, hg) for ck in range(NCK) for s in range(NSUB) for hg in range(2)]
            NU = len(units)
            items = [(u, il) for u in range(NU) for il in range(NS)]
            NI = len(items)

            def gtile(u, il):
                ck, s, hg = units[u]
                return ck * NT + s * NS + il

            def load_k(j):
                u, il = items[j]
                r0 = (units[u][2] * NTQ + gtile(u, il)) * 128
                S.dma("sp", kt[j % 3], ks_d[r0:r0 + 128, :], writes=[("kt", j % 3)])

            def load_v(j):
                u, il = items[j]
                r0 = (units[u][2] * NTQ + gtile(u, il)) * 128
                S.dma("sp", vt[j % 3], vs_d[r0:r0 + 128, :], writes=[("vt", j % 3)])

            def inproj_groups(u, kind):
                ck, s, hg = units[u]
                p = u % 2
                bi = hg * 4 + kind
                wt, wk = wload(w_hin[bi * 128:(bi + 1) * 128, :], 4096, 2048)
                out = []
                for il in range(NS):
                    def grp(il=il):
                        i = s * NS + il
                        pt, pk = ps()
                        for k in range(8):
                            S.op("pe", lambda e, k=k: e.matmul(pt[:], lhsT=hnT[:, k, i * 128:(i + 1) * 128],
                                                               rhs=wt[:, k * 512:(k + 1) * 512], start=(k == 0), stop=(k == 7)),
                                 reads=[wk, ("hn", k, i)], writes=[pk])
                        dst = qs[p] if kind == 0 else gs[p]
                        S.op("act", lambda e: e.activation(out=dst[:, il, :], in_=pt[:], func=AF.Silu),
                             reads=[pk], writes=[("qs" if kind == 0 else "gs", p, il)])
                        if kind == 3:
                            S.op("dve", lambda e: e.tensor_tensor(out=v3(dst[:, il, :]), in0=v3(dst[:, il, :]),
                                                                  in1=hnw[:].unsqueeze(1).broadcast_to([128, 4, 128]), op=ALU.mult),
                                 reads=[("gs", p, il), "hnw"], writes=[("gs", p, il)])
                    out.append(grp)
                return out

            def f1(j):
                b, b3, kb = j % 2, j % 3, j % 3
                S.op("act", lambda e: e.activation(out=logf, in_=kt[kb], func=AF.Ln, scale=-1.0, bias=1.0),
                     reads=[("kt", kb)], writes=["logf"])
                S.op("act", lambda e: e.activation(out=lhi, in_=kt[kb], func=AF.Ln, scale=-1.0, bias=1.0),
                     reads=[("kt", kb)], writes=["lhi"])
                yield
                S.op("dve", lambda e: e.tensor_tensor(out=llo, in0=logf, in1=lhi, op=ALU.subtract),
                     reads=["logf", "lhi"], writes=["llo"])
                yield
                pb, pbk = ps()
                S.op("pe", lambda e: e.matmul(pb[:], lhsT=Lmb, rhs=lhi, start=True, stop=False), reads=["lhi", "matsb"], writes=[pbk])
                S.op("pe", lambda e: e.matmul(pb[:], lhsT=Lmb, rhs=llo, start=False, stop=True), reads=["llo", "matsb"], writes=[pbk])
                S.op("act", lambda e: e.activation(out=eb[b], in_=pb[:], func=AF.Exp), reads=[pbk], writes=[("eb", b)])
                S.op("act", lambda e: e.activation(out=enb[b], in_=pb[:], func=AF.Exp, scale=-1.0), reads=[pbk], writes=[("enb", b)])
                yield
                pr, prk = ps()
                S.op("pe", lambda e: e.matmul(pr[:], lhsT=Umb, rhs=lhi, start=True, stop=False), reads=["lhi", "matsb"], writes=[prk])
                S.op("pe", lambda e: e.matmul(pr[:], lhsT=Umb, rhs=llo, start=False, stop=True), reads=["llo", "matsb"], writes=[prk])
                S.op("act", lambda e: e.activation(out=eR[b], in_=pr[:], func=AF.Exp), reads=[prk], writes=[("eR", b)])
                yield
                pd, pdk = ps()
                for hh in range(4):
                    S.op("pe", lambda e, hh=hh: e.matmul(pd[:, 2 * hh:2 * hh + 2], lhsT=lhi[:, hh * 128:(hh + 1) * 128],
                                                         rhs=blkb, start=True, stop=False), reads=["lhi", "matsb"], writes=[pdk])
                    S.op("pe", lambda e, hh=hh: e.matmul(pd[:, 2 * hh:2 * hh + 2], lhsT=llo[:, hh * 128:(hh + 1) * 128],
                                                         rhs=blkb, start=False, stop=True), reads=["llo", "matsb"], writes=[pdk])
                S.op("act", lambda e: e.activation(out=dS[b3], in_=pd[:, 0:8], func=AF.Exp), reads=[pdk], writes=[("dS", b3)])

            def f2(j):
                u, il = items[j]
                p, b, kb = u % 2, j % 2, j % 3
                S.op("dve", lambda e: e.tensor_tensor(out=kd[b], in0=kt[kb], in1=eR[b], op=ALU.mult),
                     reads=[("kt", kb), ("eR", b)], writes=[("kd", b)])
                S.op("dve", lambda e: e.tensor_tensor(out=qe, in0=qs[p][:, il, :], in1=eb[b], op=ALU.mult),
                     reads=[("qs", p, il), ("eb", b)], writes=["qe"])
                yield
                S.op("dve", lambda e: e.tensor_tensor(out=ke, in0=kt[kb], in1=enb[b], op=ALU.mult),
                     reads=[("kt", kb), ("enb", b)], writes=["ke"])
                pq, pqk = ps()
                for hh in range(4):
                    S.op("pe", lambda e, hh=hh: e.matmul(pq[:, hh * 128:(hh + 1) * 128], lhsT=qe[:, hh * 128:(hh + 1) * 128],
                                                         rhs=identb[:], start=True, stop=True), reads=["qe", "identb"], writes=[pqk])
                S.op("act", lambda e: e.activation(out=qeT[b], in_=pq[:], func=AF.Copy), reads=[pqk], writes=[("qeT", b)])
                yield
                pk_, pkk = ps()
                for hh in range(4):
                    S.op("pe", lambda e, hh=hh: e.matmul(pk_[:, hh * 128:(hh + 1) * 128], lhsT=ke[:, hh * 128:(hh + 1) * 128],
                                                         rhs=identb[:], start=True, stop=True), reads=["ke", "identb"], writes=[pkk])
                S.op("act", lambda e: e.activation(out=keT, in_=pk_[:], func=AF.Copy), reads=[pkk], writes=["keT"])
                yield
                psc, psck = ps()
                for hh in range(4):
                    S.op("pe", lambda e, hh=hh: e.matmul(psc[:, hh * 128:(hh + 1) * 128], lhsT=keT[:, hh * 128:(hh + 1) * 128],
                                                         rhs=qeT[b][:, hh * 128:(hh + 1) * 128], start=True, stop=True),
                         reads=["keT", ("qeT", b)], writes=[psck])
                S.op("dve", lambda e: e.tensor_tensor(out=v3(scT[b]), in0=v3(psc[:]),
                                                      in1=maskT.unsqueeze(1).broadcast_to([128, 4, 128]), op=ALU.mult),
                     reads=[psck, "mats"], writes=[("scT", b)])

            def b1(j):
                u, il = items[j]
                hg = units[u][2]
                b, b3, vb_ = j % 2, j % 3, j % 3
                v = vt[vb_]
                vk = ("vt", vb_)
                pua, puak = ps_pin_u(0)
                pub, pubk = ps_pin_u(1)
                for hh in range(4):
                    cs = slice(hh * 128, (hh + 1) * 128)
                    S.op("pe", lambda e, cs=cs: e.matmul(pua[:, cs], lhsT=kd[b][0:64, cs], rhs=v[0:64, cs], start=True, stop=True),
                         reads=[("kd", b), vk], writes=[puak])
                    S.op("pe", lambda e, cs=cs: e.matmul(pub[:, cs], lhsT=kd[b][64:128, cs], rhs=v[64:128, cs], start=True, stop=True),
                         reads=[("kd", b), vk], writes=[pubk])
                S.op("act", lambda e: e.activation(out=SAb, in_=Sst[:, hg * 512:(hg + 1) * 512], func=AF.Copy),
                     reads=[("S", hg * 4 + hh) for hh in range(4)], writes=[("SAb", hh) for hh in range(4)])
                yield
                for hh in range(4):
                    h = hg * 4 + hh
                    cs = slice(hh * 128, (hh + 1) * 128)
                    Sh = Sst[:, h * 128:(h + 1) * 128]
                    S.op("dve", lambda e, cs=cs, Sh=Sh, hh=hh: e.scalar_tensor_tensor(
                        out=SBf[:, cs], in0=Sh, scalar=dS[b3][:, 2 * hh:2 * hh + 1], in1=pua[:, cs], op0=ALU.mult, op1=ALU.add),
                        reads=[("S", h), ("dS", b3), puak, ("SAb", hh)], writes=[("SBf", hh)])
                    S.op("dve", lambda e, cs=cs, Sh=Sh, hh=hh: e.scalar_tensor_tensor(
                        out=Sh, in0=SBf[:, cs], scalar=dS[b3][:, 2 * hh + 1:2 * hh + 2], in1=pub[:, cs], op0=ALU.mult, op1=ALU.add),
                        reads=[("SBf", hh), ("dS", b3), pubk], writes=[("S", h)])
                    if hh % 2 == 1:
                        yield
                S.op("act", lambda e: e.activation(out=SBb, in_=SBf, func=AF.Copy),
                     reads=[("SBf", hh) for hh in range(4)], writes=[("SBb", hh) for hh in range(4)])
                po, pok = ps_pin(b)
                for hh in range(4):
                    cs = slice(hh * 128, (hh + 1) * 128)
                    S.op("pe", lambda e, cs=cs: e.matmul(po[:, cs], lhsT=scT[b][:, cs], rhs=v[:, cs], start=True, stop=False),
                         reads=[("scT", b), vk], writes=[pok])
                    S.op("pe", lambda e, cs=cs, hh=hh: e.matmul(po[0:64, cs], lhsT=qeT[b][:, hh * 128:hh * 128 + 64],
                                                                rhs=SAb[:, cs], start=False, stop=False),
                         reads=[("qeT", b), ("SAb", hh)], writes=[pok])
                    S.op("pe", lambda e, cs=cs, hh=hh: e.matmul(po[64:128, cs], lhsT=qeT[b][:, hh * 128 + 64:hh * 128 + 128],
                                                                rhs=SBb[:, cs], start=False, stop=True),
                         reads=[("qeT", b), ("SBb", hh)], writes=[pok])

            def b2(j):
                u, il = items[j]
                hg = units[u][2]
                p = u % 2
                po, pok = ps_pin(j % 2)
                S.op("act", lambda e: e.activation(out=sq, in_=po[:], func=AF.Square), reads=[pok], writes=["sq"])
                yield
                S.op("dve", lambda e: e.tensor_reduce(out=ssq, in_=v3(sq), axis=AX.X, op=ALU.add), reads=["sq"], writes=["ssq"])
                yield
                S.op("act", lambda e: e.activation(out=rst, in_=ssq, func=AF.Ln, scale=1.0 / 128, bias=EPS), reads=["ssq"], writes=["rst"])
                S.op("act", lambda e: e.activation(out=rst, in_=rst, func=AF.Exp, scale=-0.5), reads=["rst"], writes=["rst"])
                yield
                S.op("dve", lambda e: e.tensor_tensor(out=v3(on), in0=v3(po[:]), in1=rst.unsqueeze(2).broadcast_to([128, 4, 128]),
                                                      op=ALU.mult), reads=[pok, "rst"], writes=["on"])
                yield
                S.op("dve", lambda e: e.tensor_tensor(out=og, in0=on, in1=gs[p][:, il, :], op=ALU.mult),
                     reads=["on", ("gs", p, il)], writes=["og"])
                yield
                pg, pgk = ps()
                for hh in range(4):
                    S.op("pe", lambda e, hh=hh: e.matmul(pg[:, hh * 128:(hh + 1) * 128], lhsT=og[:, hh * 128:(hh + 1) * 128],
                                                         rhs=identb[:], start=True, stop=True), reads=["og", "identb"], writes=[pgk])
                S.op("act", lambda e: e.activation(out=ogT[:, hg * 4:(hg + 1) * 4, il * 128:(il + 1) * 128], in_=v3(pg[:]), func=AF.Copy),
                     reads=[pgk], writes=[("ogT", hg * 4 + hh, il) for hh in range(4)])

            for kind in (0, 3):
                for grp in inproj_groups(0, kind):
                    grp()
            load_k(0)
            if NI > 1:
                load_k(1)
            sched_q = {NS * (u1 - 1): u1 for u1 in range(1, NU)}
            sched_g = {NS * (u1 - 1) + 2: u1 for u1 in range(1, NU)}
            for t in range(NI + 3):
                gens = []
                if 0 <= t - 3 < NI:
                    gens.append(b2(t - 3))
                if 0 <= t - 1 < NI:
                    gens.append(f2(t - 1))
                if t < NI:
                    gens.append(f1(t))
                if 0 <= t - 2 < NI:
                    gens.append(b1(t - 2))
                drive(*gens)
                if t + 2 < NI:
                    load_k(t + 2)
                if t < NI:
                    load_v(t)
                if t in sched_g:
                    for grp in inproj_groups(sched_g[t], 3):
                        grp()
                if t in sched_q:
                    u1 = sched_q[t]
                    if units[u1][0] != units[u1 - 1][0]:
                        rmsnorm_T(0, units[u1][0])
                    for grp in inproj_groups(u1, 0):
                        grp()
                jb = t - 3
                if 0 <= jb < NI:
                    u, il = items[jb]
                    if il == NS - 1 and units[u][2] == 1:
                        ck, s, _ = units[u]
                        out_proj(w_hout, ogT, "ogT", [(i_, ck * NT + s * NS + i_) for i_ in range(NS)])
            rot[0] = NROT

        def hgrn_state_pass():
            phase_barrier("hstate")
            ar_off[0] = 0
            kst4 = [[carve(NS * 512).rearrange("p (i c) -> p i c", c=512) for _ in range(2)] for _ in range(2)]
            vb4 = [[carve(NS * 256, BF16).rearrange("p (i c) -> p i c", c=512) for _ in range(2)] for _ in range(2)]
            logf = [carve(512) for _ in range(2)]
            lhi = [carve(256, BF16) for _ in range(2)]
            llo = [carve(256, BF16) for _ in range(2)]
            eR = [carve(512) for _ in range(2)]
            kd = [[carve(256, BF16), carve(256, BF16)] for _ in range(2)]
            dS = [[carve(8), carve(8)] for _ in range(2)]
            units = [(ck, s, hg) for ck in range(NCK) for s in range(NSUB) for hg in range(2)]
            NU = len(units)

            def inproj_gen(u):
                ck, s, hg = units[u]
                wp = (u // 2) % 2
                kst, vb = kst4[wp], vb4[wp]
                i0 = s * NS
                for kind in (1, 2):
                    bi = hg * 4 + kind
                    wt, wk = wload(w_hin[bi * 128:(bi + 1) * 128, :], 4096, 2048)
                    for il in range(NS):
                        i = i0 + il
                        pt, pk = ps()
                        for k in range(8):
                            S.op("pe", lambda e, pt=pt, k=k, i=i, wt=wt: e.matmul(
                                pt[:], lhsT=hnT[:, k, i * 128:(i + 1) * 128], rhs=wt[:, k * 512:(k + 1) * 512],
                                start=(k == 0), stop=(k == 7)), reads=[wk, ("hn", k, i)], writes=[pk])
                        r0 = (hg * NTQ + ck * NT + i) * 128
                        if kind == 1:
                            S.op("act", lambda e, pt=pt, il=il: e.activation(out=kst[hg][:, il, :], in_=pt[:], func=AF.Sigmoid,
                                                                            scale=-1.0), reads=[pk], writes=[("k", wp, hg, il)])
                            S.op("dve", lambda e, il=il: e.tensor_tensor(out=kst[hg][:, il, :], in0=kst[hg][:, il, :],
                                                                        in1=oml[:, hg * 512:(hg + 1) * 512], op=ALU.mult),
                                 reads=[("k", wp, hg, il), "oml"], writes=[("k", wp, hg, il)])
                            S.dma("sp", ks_d[r0:r0 + 128, :], kst[hg][:, il, :], reads=[("k", wp, hg, il)], writes=[("ksd", hg, il)])
                        else:
                            S.op("act", lambda e, pt=pt, il=il: e.activation(out=vb[hg][:, il, :], in_=pt[:], func=AF.Copy),
                                 reads=[pk], writes=[("vb", wp, hg, il)])
                            S.dma("sp", vs_d[r0:r0 + 128, :], vb[hg][:, il, :], reads=[("vb", wp, hg, il)], writes=[("vsd", hg, il)])
                        yield

            def chain(u):
                ck, s, hg = units[u]
                wp = (u // 2) % 2
                kst, vb = kst4[wp], vb4[wp]
                for i in range(NS):
                    b = i % 2
                    S.op("act", lambda e, i=i: e.activation(out=logf[hg], in_=kst[hg][:, i, :], func=AF.Ln, scale=-1.0, bias=1.0),
                         reads=[("k", wp, hg, i)], writes=[("logf", hg)])
                    S.op("act", lambda e, i=i: e.activation(out=lhi[hg], in_=kst[hg][:, i, :], func=AF.Ln, scale=-1.0, bias=1.0),
                         reads=[("k", wp, hg, i)], writes=[("lhi", hg)])
                    yield
                    S.op("dve", lambda e: e.tensor_tensor(out=llo[hg], in0=logf[hg], in1=lhi[hg], op=ALU.subtract),
                         reads=[("logf", hg), ("lhi", hg)], writes=[("llo", hg)])
                    yield
                    pr, prk = ps()
                    S.op("pe", lambda e, pr=pr: e.matmul(pr[:], lhsT=Ufb, rhs=lhi[hg], start=True, stop=False),
                         reads=[("lhi", hg), "matsb"], writes=[prk])
                    S.op("pe", lambda e, pr=pr: e.matmul(pr[:], lhsT=Ufb, rhs=llo[hg], start=False, stop=True),
                         reads=[("llo", hg), "matsb"], writes=[prk])
                    pd, pdk = ps()
                    for hh in range(4):
                        S.op("pe", lambda e, hh=hh, pd=pd: e.matmul(pd[:, 2 * hh:2 * hh + 2], lhsT=lhi[hg][:, hh * 128:(hh + 1) * 128],
                                                                    rhs=oneb, start=True, stop=False),
                             reads=[("lhi", hg), "matsb"], writes=[pdk])
                        S.op("pe", lambda e, hh=hh, pd=pd: e.matmul(pd[:, 2 * hh:2 * hh + 2], lhsT=llo[hg][:, hh * 128:(hh + 1) * 128],
                                                                    rhs=oneb, start=False, stop=True),
                             reads=[("llo", hg), "matsb"], writes=[pdk])
                    S.op("act", lambda e, pr=pr: e.activation(out=eR[hg], in_=pr[:], func=AF.Exp), reads=[prk], writes=[("eR", hg)])
                    S.op("act", lambda e, pd=pd, b=b: e.activation(out=dS[hg][b], in_=pd[:, 0:8], func=AF.Exp),
                         reads=[pdk], writes=[("dS", hg, b)])
                    yield
                    S.op("dve", lambda e, i=i, b=b: e.tensor_tensor(out=kd[hg][b], in0=kst[hg][:, i, :], in1=eR[hg], op=ALU.mult),
                         reads=[("k", wp, hg, i), ("eR", hg)], writes=[("kd", hg, b)])
                    yield
                    pua, puak = ps()
                    for hh in range(4):
                        cs = slice(hh * 128, (hh + 1) * 128)
                        S.op("pe", lambda e, cs=cs, i=i, b=b, pua=pua: e.matmul(pua[:, cs], lhsT=kd[hg][b][:, cs],
                                                                               rhs=vb[hg][:, i, cs], start=True, stop=True),
                             reads=[("kd", hg, b), ("vb", wp, hg, i)], writes=[puak])
                    for hh in range(4):
                        h = hg * 4 + hh
                        cs = slice(hh * 128, (hh + 1) * 128)
                        Sh = Sst[:, h * 128:(h + 1) * 128]
                        S.op("dve", lambda e, cs=cs, Sh=Sh, hh=hh, b=b, pua=pua: e.scalar_tensor_tensor(
                            out=Sh, in0=Sh, scalar=dS[hg][b][:, 2 * hh:2 * hh + 1], in1=pua[:, cs],
                            op0=ALU.mult, op1=ALU.add), reads=[("S", h), ("dS", hg, b), puak], writes=[("S", h)])
                    S.op("dve", lambda e, b=b: e.tensor_tensor(out=Dt[:, hg * 4:(hg + 1) * 4], in0=Dt[:, hg * 4:(hg + 1) * 4],
                                                               in1=dS[hg][b].rearrange("p (h c) -> p h c", c=2)[:, :, 0], op=ALU.mult),
                         reads=[("dS", hg, b), ("Dt", hg)], writes=[("Dt", hg)])
                    yield

            def chain_seq(g):
                yield from g

            rmsnorm_T(0, 0)
            drive(inproj_gen(0))
            drive(inproj_gen(1))
            for w in range(NU // 2):
                nxt = []
                if 2 * w + 2 < NU:
                    if units[2 * w + 2][0] != units[2 * w][0]:
                        rmsnorm_T(0, units[2 * w + 2][0])

                    def both(w=w):
                        yield from inproj_gen(2 * w + 2)
                        yield from inproj_gen(2 * w + 3)
                    nxt = [both()]
                drive(chain(2 * w), chain(2 * w + 1), *nxt)

        def horner(G, nS, keyG):
            nh = nS // 128
            allS = [("S", h) for h in range(8)]
            Sv = Sst[:, 0:nS].rearrange("p (h d) -> p h d", h=nh)
            S.op("dve", lambda e: e.memset(Sst[:, 0:nS], 0.0), writes=allS)
            for j in range(4):
                mj = msk[:, j:j + 1]
                omj = msk[:, 4 + j:5 + j]
                S.op("dve", lambda e, j=j, mj=mj, omj=omj: e.tensor_scalar(out=dm, in0=G[:, j, nS:nS + 8], scalar1=mj, scalar2=omj,
                                                                        op0=ALU.mult, op1=ALU.add),
                     reads=[keyG, "msk"], writes=["dm"])
                S.op("dve", lambda e: e.tensor_tensor(out=Sv, in0=Sv, in1=dm[:, 0:nh].unsqueeze(2).broadcast_to([128, nh, 128]),
                                                      op=ALU.mult), reads=["dm"] + allS, writes=allS)
                S.op("dve", lambda e, j=j, mj=mj: e.scalar_tensor_tensor(out=Sst[:, 0:nS], in0=G[:, j, 0:nS], scalar=mj, in1=Sst[:, 0:nS],
                                                                        op0=ALU.mult, op1=ALU.add),
                     reads=[keyG, "msk"] + allS, writes=allS)

        def ffn_layer(l, ck):
            phase_barrier("ffn")
            ar_off[0] = 0
            actT = carve(11 * T // 2, BF16).rearrange("p (j t) -> p j t", t=T)
            sg = [carve(512), carve(512)]
            for h in range(2):
                for jj in range(11):
                    j = 11 * h + jj
                    r0 = (l * 22 + j) * 128
                    wt, wk = wload(w_fin[r0:r0 + 128, :], 2048, 2048)
                    for tb in range(NTB):
                        ts_ = slice(tb * TB, (tb + 1) * TB)
                        hk = lambda k: [("hn", k, i) for i in range(tb * TPB, (tb + 1) * TPB)]
                        pg_, pgk = ps()
                        pu_, puk = ps()
                        for k in range(8):
                            S.op("pe", lambda e, pg_=pg_, k=k, ts_=ts_, wt=wt: e.matmul(
                                pg_[:, 0:TB], lhsT=wt[:, k * 256:k * 256 + 128], rhs=hnT[:, k, ts_], start=(k == 0), stop=(k == 7)),
                                reads=[wk] + hk(k), writes=[pgk])
                        for k in range(8):
                            S.op("pe", lambda e, pu_=pu_, k=k, ts_=ts_, wt=wt: e.matmul(
                                pu_[:, 0:TB], lhsT=wt[:, k * 256 + 128:k * 256 + 256], rhs=hnT[:, k, ts_], start=(k == 0), stop=(k == 7)),
                                reads=[wk] + hk(k), writes=[puk])
                        sgt = sg[tb % 2]
                        S.op("act", lambda e, sgt=sgt, pg_=pg_: e.activation(out=sgt[:, 0:TB], in_=pg_[:, 0:TB], func=AF.Silu),
                             reads=[pgk], writes=[("sg", tb % 2)])
                        S.op("dve", lambda e, sgt=sgt, pu_=pu_, jj=jj, ts_=ts_: e.tensor_tensor(
                            out=actT[:, jj, ts_], in0=sgt[:, 0:TB], in1=pu_[:, 0:TB], op=ALU.mult),
                            reads=[("sg", tb % 2), puk], writes=[("actT", jj, tb)])
                for ch in range(2):
                    r0 = ((l * 2 + h) * 2 + ch) * 128
                    wt, wk = wload(w_fout[r0:r0 + 128, :], SLOT, 1408)
                    for i in range(NT):
                        g = ck * NT + i
                        pt, pk = ps()
                        for jj in range(11):
                            S.op("pe", lambda e, pt=pt, jj=jj, i=i, wt=wt: e.matmul(
                                pt[:], lhsT=actT[:, jj, i * 128:(i + 1) * 128], rhs=wt[:, jj * 512:(jj + 1) * 512],
                                start=(jj == 0), stop=(jj == 10)), reads=[wk, ("actT", jj, i // TPB)], writes=[pk])
                        xo = x_sb[:, g, ch * 512:(ch + 1) * 512]
                        S.op("dve", lambda e, xo=xo, pt=pt: e.tensor_tensor(out=xo, in0=xo, in1=pt[:], op=ALU.add),
                             reads=[pk, ("x", g)], writes=[("x", g)])

        lv3 = lv[:].rearrange("p (c v) -> p c v", v=8)

        def lru_sub(ck, s):
            phase_barrier("lru")
            ar_off[0] = 0
            yt = [[carve(TS), carve(TS)] for _ in range(2)]
            u = [[carve(TS + 4), carve(TS + 4)] for _ in range(2)]
            uc = [[carve(TS), carve(TS)] for _ in range(2)]
            ucb = [[carve(TS // 2, BF16), carve(TS // 2, BF16)] for _ in range(2)]
            r = [carve(TS), carve(TS)]
            ig = [carve(TS), carve(TS)]
            a2 = [carve(TS), carve(TS)]
            hs = [carve(TS), carve(TS)]
            zst = [carve(TS // 2, BF16), carve(TS // 2, BF16)]
            hyT = carve(4 * TS, BF16).rearrange("p (c t) -> p c t", t=TS)
            t0 = s * TS
            q0 = ck * T + s * TS
            hkeys = [("hn", k, i) for k in range(8) for i in range(s * NS, (s + 1) * NS)]
            wts = {}

            def load_w(n):
                wt, wk = wload(w_lin[n * 128:(n + 1) * 128, :], 4096, 2048)
                for gi in range(2):
                    S.dma("pool", wt[:, 4096 + gi * 512:4096 + (gi + 1) * 512],
                          w_ax[gi * 128:(gi + 1) * 128, n * 512:(n + 1) * 512], reads=[wk], writes=[wk])
                wts[n] = (wt, wk)

            def phase1(n, q):
                p = n % 2
                wt, wk = wts[n]
                c = 2 * n + q
                yk, uk, uck, ucbk = ("yt", p, q), ("u", p, q), ("uc", p, q), ("ucb", p, q)
                py, pyk = ps()
                for k in range(8):
                    S.op("pe", lambda e, k=k: e.matmul(
                        py[:, 0:TS], lhsT=wt[:, k * 512 + q * 128:k * 512 + q * 128 + 128], rhs=hnT[:, k, t0:t0 + TS],
                        start=(k == 0), stop=(k == 7)), reads=[wk] + [hk for hk in hkeys if hk[1] == k], writes=[pyk])
                S.op("act", lambda e: e.activation(out=yt[p][q], in_=py[:, 0:TS], func=AF.Gelu_apprx_tanh), reads=[pyk], writes=[yk])
                yield
                S.op("dve", lambda e: e.tensor_copy(out=u[p][q][:, 0:3], in_=halo[:, 3 * c:3 * c + 3]),
                     reads=[("halo", c)], writes=[uk])
                pu, puk = ps()
                for k in range(8):
                    S.op("pe", lambda e, k=k: e.matmul(
                        pu[:, 0:TS], lhsT=wt[:, k * 512 + 256 + q * 128:k * 512 + 256 + q * 128 + 128], rhs=hnT[:, k, t0:t0 + TS],
                        start=(k == 0), stop=(k == 7)), reads=[wk] + [hk for hk in hkeys if hk[1] == k], writes=[puk])
                S.op("act", lambda e: e.activation(out=u[p][q][:, 3:3 + TS], in_=pu[:, 0:TS], func=AF.Copy), reads=[puk], writes=[uk])
                yield
                S.op("dve", lambda e: e.tensor_copy(out=halo[:, 3 * c:3 * c + 3], in_=u[p][q][:, TS:TS + 3]),
                     reads=[uk], writes=[("halo", c)])
                S.op("act", lambda e: e.activation(out=uc[p][q], in_=u[p][q][:, 0:TS], func=AF.Identity, scale=lv3[:, c, 0:1],
                                                   bias=lv3[:, c, 4:5]), reads=[uk, "lv"], writes=[uck])
                yield
                for tap in range(1, 4):
                    S.op("dve", lambda e, tap=tap: e.scalar_tensor_tensor(
                        out=uc[p][q], in0=u[p][q][:, tap:tap + TS], scalar=lv3[:, c, tap:tap + 1], in1=uc[p][q],
                        op0=ALU.mult, op1=ALU.add), reads=[uk, uck, "lv"], writes=[uck])
                    yield
                S.op("act", lambda e: e.activation(out=ucb[p][q], in_=uc[p][q], func=AF.Copy), reads=[uck], writes=[ucbk])

            def phase2(n, q):
                p = n % 2
                wt, wk = wts[n]
                d = 2 * n + q
                yk, uck = ("yt", p, q), ("uc", p, q)
                pa, pak = ps()
                px, pxk = ps()
                for gi, pp in ((0, pa), (1, px)):
                    for qq in range(2):
                        o0 = 4096 + gi * 512 + qq * 256 + q * 128
                        S.op("pe", lambda e, pp=pp, qq=qq, o0=o0: e.matmul(
                            pp[:, 0:TS], lhsT=wt[:, o0:o0 + 128], rhs=ucb[p][qq], start=(qq == 0), stop=(qq == 1)),
                            reads=[wk, ("ucb", p, qq)], writes=[pak if gi == 0 else pxk])
                S.op("act", lambda e: e.activation(out=r[q], in_=pa[:, 0:TS], func=AF.Tanh, scale=0.5, bias=scl[:, 16 + d:17 + d]),
                     reads=[pak, "scl"], writes=[("r", q)])
                S.op("act", lambda e: e.activation(out=ig[q], in_=px[:, 0:TS], func=AF.Tanh, scale=0.5, bias=scl[:, 24 + d:25 + d]),
                     reads=[pxk, "scl"], writes=[("ig", q)])
                yield
                S.op("act", lambda e: e.activation(out=a2[q], in_=r[q], func=AF.Exp, scale=scl[:, d:d + 1], bias=scl[:, d:d + 1]),
                     reads=[("r", q), "scl"], writes=[("a2", q)])
                S.op("act", lambda e: e.activation(out=r[q], in_=r[q], func=AF.Exp, scale=scl[:, 8 + d:9 + d], bias=scl[:, 8 + d:9 + d]),
                     reads=[("r", q), "scl"], writes=[("r", q)])
                yield
                S.op("dve", lambda e: e.tensor_scalar(out=a2[q], in0=a2[q], scalar1=0.99999994, scalar2=None, op0=ALU.min),
                     reads=[("a2", q)], writes=[("a2", q)])
                yield
                S.op("act", lambda e: e.activation(out=a2[q], in_=a2[q], func=AF.Ln, scale=-1.0, bias=1.0),
                     reads=[("a2", q)], writes=[("a2", q)])
                S.op("act", lambda e: e.activation(out=a2[q], in_=a2[q], func=AF.Exp, scale=0.5),
                     reads=[("a2", q)], writes=[("a2", q)])
                yield
                S.op("dve", lambda e: e.scalar_tensor_tensor(out=ig[q], in0=ig[q], scalar=1.0, in1=a2[q], op0=ALU.add, op1=ALU.mult),
                     reads=[("a2", q), ("ig", q)], writes=[("ig", q)])
                S.op("dve", lambda e: e.scalar_tensor_tensor(out=ig[q], in0=ig[q], scalar=0.5, in1=uc[p][q], op0=ALU.mult, op1=ALU.mult),
                     reads=[("ig", q), uck], writes=[("ig", q)])
                yield
                S.op("dve", lambda e: e.tensor_tensor_scan(out=hs[q], data0=r[q], data1=ig[q], initial=hloc[:, d:d + 1],
                                                           op0=ALU.mult, op1=ALU.add),
                     reads=[("r", q), ("ig", q), ("hloc", d)], writes=[("hs", q)])
                yield
                S.op("dve", lambda e: e.tensor_copy(out=hloc[:, d:d + 1], in_=hs[q][:, TS - 1:TS]),
                     reads=[("hs", q)], writes=[("hloc", d)])
                S.op("dve", lambda e: e.tensor_tensor(out=hyT[:, d, :], in0=hs[q], in1=yt[p][q], op=ALU.mult),
                     reads=[("hs", q), yk], writes=[("hyT", d, il) for il in range(NS)])
                yield
                S.op("dve", lambda e: e.tensor_tensor_scan(out=hs[q], data0=r[q], data1=ig[q], initial=pcs[:, d:d + 1],
                                                           op0=ALU.mult, op1=ALU.bypass),
                     reads=[("r", q), ("ig", q), ("pcs", d)] + [("hyT", d, il) for il in range(NS)], writes=[("hs", q)])
                yield
                S.op("dve", lambda e: e.tensor_copy(out=pcs[:, d:d + 1], in_=hs[q][:, TS - 1:TS]),
                     reads=[("hs", q)], writes=[("pcs", d)])
                S.op("dve", lambda e: e.tensor_tensor(out=zst[q], in0=hs[q], in1=yt[p][q], op=ALU.mult),
                     reads=[("hs", q), yk], writes=[("zst", q)])
                S.dma("sp", zd[d * 128:(d + 1) * 128, q0:q0 + TS], zst[q], reads=[("zst", q)], writes=[("zd", d, q0)])

            load_w(0)
            drive(phase1(0, 0), phase1(0, 1))
            for n in range(4):
                if n + 1 < 4:
                    load_w(n + 1)
                    drive(phase2(n, 0), phase1(n + 1, 0), phase2(n, 1), phase1(n + 1, 1))
                else:
                    drive(phase2(n, 0), phase2(n, 1))
            out_proj(w_lout, hyT, "hyT", [(il, ck * NT + s * NS + il) for il in range(NS)])

        def allgather(src_sb, ncols, ag_in, ag_out, G, tag, src_keys):
            S.dma("sp", ag_in, src_sb, reads=src_keys, writes=[tag + "_in"])
            S.coll(lambda e: e.collective_compute("AllGather", ALU.bypass, replica_groups=RG, ins=[ag_in], outs=[ag_out]),
                   reads=[tag + "_in"], writes=[tag + "_out"])
            S.dma("sp", G, ag_out.rearrange("(r p) c -> p r c", p=128), reads=[tag + "_out"], writes=[tag + "_G"])

        def final_norm_store(tiles):
            for n_, g in enumerate(tiles):
                b = n_ % 2
                ss = ssq1[:, 4 + 2 * b:5 + 2 * b]
                rs = ssq1[:, 5 + 2 * b:6 + 2 * b]
                S.op("act", lambda e, g=g, ss=ss, b=b: e.activation(out=xsb[b], in_=x_sb[:, g, :], func=AF.Square, accum_out=ss),
                     reads=[("x", g)], writes=[("xs", b), ("fss", b)])
                S.op("act", lambda e, ss=ss, rs=rs: e.activation(out=rs, in_=ss, func=AF.Sqrt, scale=1.0 / D, bias=EPS),
                     reads=[("fss", b)], writes=[("frs", b)])
                S.op("dve", lambda e, rs=rs: e.reciprocal(out=rs, in_=rs), reads=[("frs", b)], writes=[("frs", b)])
                S.op("dve", lambda e, g=g, rs=rs: e.scalar_tensor_tensor(out=x_sb[:, g, :], in0=x_sb[:, g, :], scalar=rs, in1=nfin[:],
                                                                        op0=ALU.mult, op1=ALU.mult),
                     reads=[("x", g), ("frs", b), "nfin"], writes=[("x", g)])
                S.dma("sp", out_d[g * 128:(g + 1) * 128, :], x_sb[:, g, :], reads=[("x", g)], writes=[("out", g)])

        if STAGE >= 2:
            hgrn_state_pass()
            S.barrier()
            ar_off[0] = 0
            G1 = carve(4 * NG1).rearrange("p (r c) -> p r c", c=NG1)
            S.dma("sp", ag1_in, Sst[:], reads=[("S", h) for h in range(8)] + [("Dt", 0), ("Dt", 1)], writes=["ag1_in"])
            S.coll(lambda e: e.collective_compute("AllGather", ALU.bypass, replica_groups=RG, ins=[ag1_in], outs=[ag1_out]),
                   reads=["ag1_in"], writes=["ag1_out"])
            rmsnorm_T(0, 0)
            S.dma("sp", G1, ag1_out.rearrange("(r p) c -> p r c", p=128), reads=["ag1_out"], writes=["ag1_G"])
            horner(G1, 1024, "ag1_G")
            hgrn_full()
        hl = sm[:, 64:88]
        G2 = sm[:, 88:184].rearrange("p (r c) -> p r c", c=24)

        def halo_prepass():
            rmsnorm_T(2, NCK - 1, tiles=[NT - 1])
            for n in range(4):
                wt, wk = wload(w_lin[n * 128:(n + 1) * 128, :], 4096, 2048)
                for q in range(2):
                    c = 2 * n + q
                    pu, puk = ps()
                    for k in range(8):
                        S.op("pe", lambda e, pu=pu, k=k, q=q, wt=wt: e.matmul(
                            pu[:, 0:128], lhsT=wt[:, k * 512 + 256 + q * 128:k * 512 + 256 + q * 128 + 128],
                            rhs=hnT[:, k, (NT - 1) * 128:NT * 128], start=(k == 0), stop=(k == 7)),
                            reads=[wk, ("hn", k, NT - 1)], writes=[puk])
                    S.op("act", lambda e, pu=pu, c=c: e.activation(out=hl[:, 3 * c:3 * c + 3], in_=pu[:, 125:128], func=AF.Copy),
                         reads=[puk], writes=["ag2_src"])
            S.dma("sp", ag2_in, hl, reads=["ag2_src"], writes=["ag2_in"])
            S.coll(lambda e: e.collective_compute("AllGather", ALU.bypass, replica_groups=RG, ins=[ag2_in], outs=[ag2_out]),
                   reads=["ag2_in"], writes=["ag2_out"])

        if STAGE >= 3:
            for ck in reversed(range(NCK)):
                rmsnorm_T(1, ck)
                ffn_layer(0, ck)
                if ck == NCK - 1 and STAGE >= 4:
                    halo_prepass()
        if STAGE >= 4:
            S.dma("sp", G2, ag2_out.rearrange("(r p) c -> p r c", p=128), reads=["ag2_out"], writes=["ag2_G"])
            S.op("dve", lambda e: e.tensor_scalar(out=halo, in0=G2[:, 0, :], scalar1=msk[:, 8:9], scalar2=None, op0=ALU.mult),
                 reads=["ag2_G", "msk"], writes=[("halo", c) for c in range(8)])
            for j in range(1, 4):
                S.op("dve", lambda e, j=j: e.scalar_tensor_tensor(out=halo, in0=G2[:, j, :], scalar=msk[:, 8 + j:9 + j], in1=halo,
                                                                 op0=ALU.mult, op1=ALU.add),
                     reads=["ag2_G", "msk"] + [("halo", c) for c in range(8)], writes=[("halo", c) for c in range(8)])
            for ck in range(NCK):
                rmsnorm_T(2, ck)
                for s in range(NSUB):
                    lru_sub(ck, s)
            S.barrier()
            ar_off[0] = 0
            G3 = carve(4 * 16).rearrange("p (r c) -> p r c", c=16)
            src3 = carve(16)
            Zt = [carve(4 * 512, BF16).rearrange("p (c t) -> p c t", t=512) for _ in range(2)]
            S.op("dve", lambda e: e.tensor_copy(out=src3[:, 0:8], in_=hloc), reads=[("hloc", d) for d in range(8)], writes=["ag3_src"])
            S.op("dve", lambda e: e.tensor_copy(out=src3[:, 8:16], in_=pcs), reads=[("pcs", d) for d in range(8)] + ["ag3_src"],
                 writes=["ag3_src"])
            allgather(src3, 16, ag3_in, ag3_out, G3, "ag3", ["ag3_src"])
            S.op("dve", lambda e: e.memset(hin, 0.0), writes=["hin"])
            for j in range(4):
                mj = msk[:, j:j + 1]
                omj = msk[:, 4 + j:5 + j]
                S.op("dve", lambda e, j=j, mj=mj, omj=omj: e.tensor_scalar(out=dm, in0=G3[:, j, 8:16], scalar1=mj, scalar2=omj,
                                                                        op0=ALU.mult, op1=ALU.add),
                     reads=["ag3_G", "msk"], writes=["dm"])
                S.op("dve", lambda e: e.tensor_tensor(out=hin, in0=hin, in1=dm, op=ALU.mult), reads=["dm", "hin"], writes=["hin"])
                S.op("dve", lambda e, j=j, mj=mj: e.scalar_tensor_tensor(out=hin, in0=G3[:, j, 0:8], scalar=mj, in1=hin,
                                                                        op0=ALU.mult, op1=ALU.add),
                     reads=["ag3_G", "msk", "hin"], writes=["hin"])
            wh = [wload(w_lout[half * 128:(half + 1) * 128, :], 4096, 2048) for half in range(2)]
            GT = min(512, QL)
            zsrc = zd.rearrange("(c p) t -> p c t", p=128)
            for tg in range(QL // GT):
                zt = Zt[tg % 2]
                zk = ("Zt", tg % 2)
                S.dma("sp", zt[:, :, 0:GT], zsrc[:, :, tg * GT:(tg + 1) * GT], writes=[zk])
                for d in range(8):
                    S.op("dve", lambda e, d=d, zt=zt: e.tensor_scalar(out=zt[:, d, 0:GT], in0=zt[:, d, 0:GT], scalar1=hin[:, d:d + 1],
                                                                     scalar2=None, op0=ALU.mult), reads=["hin", zk], writes=[zk])
                for half in range(2):
                    wt, wk = wh[half]
                    for il in range(GT // 128):
                        g = tg * (GT // 128) + il
                        pt, pk = ps()
                        for k in range(8):
                            S.op("pe", lambda e, pt=pt, k=k, il=il, wt=wt, zt=zt: e.matmul(
                                pt[:], lhsT=zt[:, k, il * 128:(il + 1) * 128], rhs=wt[:, k * 512:(k + 1) * 512],
                                start=(k == 0), stop=(k == 7)), reads=[wk, zk], writes=[pk])
                        xo = x_sb[:, g, half * 512:(half + 1) * 512]
                        S.op("dve", lambda e, xo=xo, pt=pt: e.tensor_tensor(out=xo, in0=xo, in1=pt[:], op=ALU.add),
                             reads=[pk, ("x", g)], writes=[("x", g)])
        if STAGE >= 5:
            for ck in range(NCK):
                rmsnorm_T(3, ck)
                ffn_layer(1, ck)
                final_norm_store(range(ck * NT, (ck + 1) * NT))
        else:
            S.barrier()
            final_norm_store(range(NTQ))
        S.barrier()
        S.emit_all()
    return nc


def _tile_k(w, cols):
    sub = w[:, cols]
    C = sub.shape[1]
    return np.ascontiguousarray(sub.reshape(8, 128, C).transpose(1, 0, 2).reshape(128, 8 * C))


def prep_weights(inp):
    f = np.float32
    hw = np.asarray(inp["hgrn_w_in"], f)[0]
    w_hin = np.concatenate([_tile_k(hw, np.arange(kind * 1024 + hg * 512, kind * 1024 + hg * 512 + 512))
                            for hg in range(2) for kind in range(4)], axis=0)
    ho = np.asarray(inp["hgrn_w_out"], f)[0]
    w_hout = np.concatenate([_tile_k(ho, np.arange(h * 512, h * 512 + 512)) for h in range(2)], axis=0)
    fi = np.asarray(inp["ffn_w_in"], f)
    w_fin = np.concatenate([_tile_k(fi[l], np.concatenate([np.arange(j * 128, j * 128 + 128),
                                                           np.arange(DFF + j * 128, DFF + j * 128 + 128)]))
                            for l in range(2) for j in range(22)], axis=0)
    fo = np.asarray(inp["ffn_w_out"], f)
    blocks = []
    for l in range(2):
        for h in range(2):
            for ch in range(2):
                sub = fo[l][h * 1408:(h + 1) * 1408, ch * 512:(ch + 1) * 512]
                blocks.append(sub.reshape(11, 128, 512).transpose(1, 0, 2).reshape(128, SLOT))
    w_fout = np.ascontiguousarray(np.concatenate(blocks, axis=0))
    li = np.asarray(inp["lru_w_in"], f)[0]
    w_lin = np.concatenate([_tile_k(li, np.concatenate([np.arange(n * 256, n * 256 + 256),
                                                        np.arange(1024 + n * 256, 1024 + n * 256 + 256)]))
                            for n in range(4)], axis=0)
    lo = np.asarray(inp["lru_w_out"], f)[0]
    w_lout = np.concatenate([_tile_k(lo, np.arange(h * 512, h * 512 + 512)) for h in range(2)], axis=0)
    ax = []
    for nm in ("lru_wa", "lru_wx"):
        w = np.asarray(inp[nm], f)[0]
        ax.append(w.reshape(8, 128, 256).transpose(1, 0, 2).reshape(128, 2048))
    w_ax = np.ascontiguousarray(np.concatenate(ax, axis=0))
    s = np.arange(128)[:, None]
    t = np.arange(128)[None, :]
    same = (s // 64) == (t // 64)
    ident = np.eye(128, dtype=f)
    Lm = (same & (s <= t)).astype(f)
    Um = (same & (s > t)).astype(f)
    blk = np.stack([(np.arange(128) < 64), (np.arange(128) >= 64)], axis=1).astype(f)
    Uf = (s > t).astype(f)
    ones2 = np.ones((128, 2), f)
    c_mats = np.ascontiguousarray(np.concatenate([ident, Lm, Um, Lm, blk, Uf, ones2], axis=1))
    assert c_mats.shape == (128, 644)
    nm_, nf_ = np.asarray(inp["norm_mix"], f), np.asarray(inp["norm_ffn"], f)
    rows = [nm_[0], nf_[0], nm_[1], nf_[1]]
    c_nrm = np.ascontiguousarray(np.stack([r.reshape(8, 128).T for r in rows], axis=1).reshape(128, 32))
    c_nrmb = np.ascontiguousarray(np.concatenate([np.broadcast_to(r.reshape(1, 1024), (128, 1024)) for r in rows], axis=0))
    vals = [np.asarray(inp["lru_conv_w"], f)[0][i] for i in range(4)] + \
           [np.asarray(inp[k], f)[0] for k in ("lru_conv_b", "lru_ba", "lru_bx", "lru_lambda")]
    c_lv = np.ascontiguousarray(np.stack([v.reshape(8, 128).T for v in vals], axis=2).reshape(128, 64))
    lb = np.asarray(inp["hgrn_lb"], f)
    c_lb = np.ascontiguousarray(np.broadcast_to(lb.reshape(1, 2048), (128, 2048)))
    c_hnw = np.ascontiguousarray(np.broadcast_to(np.asarray(inp["hgrn_norm"], f).reshape(1, 128), (128, 128)))
    c_nfin = np.ascontiguousarray(np.broadcast_to(np.asarray(inp["norm_final"], f).reshape(1, 1024), (128, 1024)))
    return dict(w_hin=w_hin, w_hout=w_hout, w_fin=w_fin, w_fout=w_fout, w_lin=w_lin, w_lout=w_lout, w_ax=w_ax,
                c_mats=c_mats, c_nrm=c_nrm, c_nrmb=c_nrmb, c_lv=c_lv, c_lb=c_lb, c_hnw=c_hnw, c_nfin=c_nfin)


def run(inp, T=1024, NCK=2):
    x = np.asarray(inp["x"], np.float32)
    B, SEQ, _ = x.shape
    QL = T * NCK
    assert SEQ == 4 * QL and B == 2
    nc = build(T, NCK)
    wd = prep_weights(inp)
    in_maps = []
    for c in range(8):
        b, q = c // 4, c % 4
        m = np.zeros(16, np.float32)
        for j in range(4):
            m[j] = 1.0 if j < q else 0.0
            m[4 + j] = 1.0 - m[j]
            m[8 + j] = 1.0 if j == q - 1 else 0.0
        msk = np.ascontiguousarray(np.broadcast_to(m.reshape(1, 16), (128, 16)))
        in_maps.append(dict(wd, x=np.ascontiguousarray(x[b, q * QL:(q + 1) * QL]), c_msk=msk))
    res = run_bass_kernel_spmd(nc, in_maps, core_ids=list(range(8)))
    out = np.empty((B, SEQ, D), np.float32)
    for c in range(8):
        b, q = c // 4, c % 4
        out[b, q * QL:(q + 1) * QL] = res.results[c]["out"]
    return out


def kernel(**inputs):
    return run(inputs, T=1024, NCK=2)
```
